# Optimizing a Trainium2 kernel written in Bass

```python
import jax, jax.numpy as jnp
from jax import lax
import numpy as np

D_MODEL = 4096
BATCH = 4
SEQ = 4096
DEPTH = 4

GRID_W = 64
CTX_LEN = 256
N_MIXERS = 4
HEAD_DIM = 128
N_HEADS = D_MODEL // HEAD_DIM
N_KV_HEADS = N_HEADS // 4
Q_PER_KV = N_HEADS // N_KV_HEADS
QKV_DIM = (N_HEADS + 2 * N_KV_HEADS) * HEAD_DIM
Q_BLOCK = 128
WINDOW = 128
NA_KH = 8
NA_KW = 16
MLA_Q_RANK = D_MODEL // 4
MLA_KV_RANK = D_MODEL // 8
MLA_NOPE = 128
MLA_ROPE = 64
MLA_V = 128
D_FF = 3 * D_MODEL // 2
CONV_W = 3
ADA_CHUNKS = 6
ROPE_THETA = 10000.0
EPS = 1e-6
NEG_INF = -1e30

kernel_name = "hybrid_interleaved_diffusion_trunk"


def rms_norm(x, g):
    xf = x.astype(jnp.float32)
    y = xf * lax.rsqrt(jnp.mean(xf * xf, axis=-1, keepdims=True) + EPS)
    return (y * g.astype(jnp.float32)).astype(x.dtype)


def modulate(x, g, shift, scale):
    return rms_norm(x, g) * (1 + scale) + shift


def axial_rope_tables(n_tok, rot_dim):
    n_freq = rot_dim // 4
    freqs = ROPE_THETA ** (-jnp.arange(n_freq, dtype=jnp.float32) / n_freq)
    t = jnp.arange(n_tok)
    row = (t // GRID_W).astype(jnp.float32)
    col = (t % GRID_W).astype(jnp.float32)
    ang = jnp.concatenate([row[:, None] * freqs, col[:, None] * freqs], axis=-1)
    return jnp.cos(ang), jnp.sin(ang)


def apply_rope(x, cos, sin):
    half = x.shape[-1] // 2
    shape = (cos.shape[0],) + (1,) * (x.ndim - 3) + (half,)
    cos = cos.reshape(shape)
    sin = sin.reshape(shape)
    xf = x.astype(jnp.float32)
    x1, x2 = xf[..., :half], xf[..., half:]
    return jnp.concatenate([x1 * cos - x2 * sin, x1 * sin + x2 * cos], axis=-1).astype(x.dtype)


def gqa_project(h, w_qkv):
    b, n, _ = h.shape
    q, k, v = jnp.split(h @ w_qkv, [N_HEADS * HEAD_DIM, (N_HEADS + N_KV_HEADS) * HEAD_DIM], axis=-1)
    return (q.reshape(b, n, N_KV_HEADS, Q_PER_KV, HEAD_DIM),
            k.reshape(b, n, N_KV_HEADS, HEAD_DIM),
            v.reshape(b, n, N_KV_HEADS, HEAD_DIM))


def attend(q, k, v, scale, mask=None, sink=None):
    s = jnp.einsum('bqgrd,bkgd->bgrqk', q, k).astype(jnp.float32) * scale
    if mask is not None:
        s = jnp.where(mask, s, NEG_INF)
    if sink is not None:
        sk = jnp.broadcast_to(sink.astype(jnp.float32)[None, :, :, None, None], s.shape[:-1] + (1,))
        p = jax.nn.softmax(jnp.concatenate([s, sk], axis=-1), axis=-1)[..., :-1]
    else:
        p = jax.nn.softmax(s, axis=-1)
    return jnp.einsum('bgrqk,bkgd->bqgrd', p.astype(v.dtype), v)


def to_blocks(a):
    nb = a.shape[1] // Q_BLOCK
    return a.reshape((a.shape[0], nb, Q_BLOCK) + a.shape[2:]).swapaxes(0, 1)


def from_blocks(o):
    return o.swapaxes(0, 1).reshape((o.shape[1], o.shape[0] * o.shape[2]) + o.shape[3:])


def dense_blocks(q, k, v, scale):
    return from_blocks(lax.map(lambda qb: attend(qb, k, v, scale), to_blocks(q)))


def merge_heads(o, w_o):
    return o.reshape(o.shape[0], o.shape[1], -1) @ w_o


def mixer_global(hx, hc, need_ctx, w_qkv, q_norm_g, k_norm_g, w_o):
    n_lat = hx.shape[1]
    qx, kx, vx = gqa_project(hx, w_qkv)
    qc, kc, vc = gqa_project(hc, w_qkv)
    cos, sin = axial_rope_tables(n_lat, HEAD_DIM)
    qx = apply_rope(rms_norm(qx, q_norm_g), cos, sin)
    kx = apply_rope(rms_norm(kx, k_norm_g), cos, sin)
    kc = rms_norm(kc, k_norm_g)
    scale = HEAD_DIM ** -0.5
    ox = dense_blocks(qx, jnp.concatenate([kc, kx], axis=1), jnp.concatenate([vc, vx], axis=1), scale)
    out_x = merge_heads(ox, w_o)
    out_c = merge_heads(attend(rms_norm(qc, q_norm_g), kc, vc, scale), w_o) if need_ctx else None
    return out_x, out_c


def mixer_window(hx, hc, need_ctx, w_qkv, sink, w_o):
    n_lat, n_ctx = hx.shape[1], hc.shape[1]
    qx, kx, vx = gqa_project(hx, w_qkv)
    qc, kc, vc = gqa_project(hc, w_qkv)
    cos, sin = axial_rope_tables(n_lat, HEAD_DIM)
    qx = apply_rope(qx, cos, sin)
    kx = apply_rope(kx, cos, sin)
    sink_gr = sink.reshape(N_KV_HEADS, Q_PER_KV)
    scale = HEAD_DIM ** -0.5
    span = Q_BLOCK + 2 * WINDOW
    pad = ((0, 0), (WINDOW, WINDOW), (0, 0), (0, 0))
    kpad = jnp.pad(kx, pad)
    vpad = jnp.pad(vx, pad)
    ctx_mask = jnp.ones((Q_BLOCK, n_ctx), dtype=bool)

    def block(args):
        qb, blk = args
        start = blk * Q_BLOCK
        kb = lax.dynamic_slice_in_dim(kpad, start, span, axis=1)
        vb = lax.dynamic_slice_in_dim(vpad, start, span, axis=1)
        qpos = start + jnp.arange(Q_BLOCK)
        kpos = start - WINDOW + jnp.arange(span)
        band = ((jnp.abs(qpos[:, None] - kpos[None, :]) <= WINDOW)
                & (kpos >= 0)[None, :] & (kpos < n_lat)[None, :])
        return attend(qb, jnp.concatenate([kc, kb], axis=1), jnp.concatenate([vc, vb], axis=1), scale,
                      mask=jnp.concatenate([ctx_mask, band], axis=1), sink=sink_gr)

    nb = n_lat // Q_BLOCK
    out_x = merge_heads(from_blocks(lax.map(block, (to_blocks(qx), jnp.arange(nb)))), w_o)
    out_c = merge_heads(attend(qc, kc, vc, scale, sink=sink_gr), w_o) if need_ctx else None
    return out_x, out_c


def mixer_neighbourhood(hx, hc, need_ctx, w_qkv, rel_bias, w_o):
    n_lat, n_ctx = hx.shape[1], hc.shape[1]
    rows = n_lat // GRID_W
    kh = min(NA_KH, rows)
    nk = kh * NA_KW
    nb = n_lat // Q_BLOCK
    qx, kx, vx = gqa_project(hx, w_qkv)
    qc, kc, vc = gqa_project(hc, w_qkv)
    scale = HEAD_DIM ** -0.5
    t = jnp.arange(n_lat)
    r, col = t // GRID_W, t % GRID_W
    rs = jnp.clip(r - kh // 2, 0, rows - kh)
    cs = jnp.clip(col - NA_KW // 2, 0, GRID_W - NA_KW)
    kr = rs[:, None] + jnp.arange(kh)
    kcol = cs[:, None] + jnp.arange(NA_KW)
    nbr = (kr[:, :, None] * GRID_W + kcol[:, None, :]).reshape(n_lat, nk)
    dr = kr - r[:, None] + (NA_KH - 1)
    dc = kcol - col[:, None] + (NA_KW - 1)
    bias = rel_bias[:, dr[:, :, None], dc[:, None, :]].astype(jnp.float32)
    bias = bias.reshape(N_KV_HEADS, Q_PER_KV, nb, Q_BLOCK, nk).transpose(2, 0, 1, 3, 4)
    idx = nbr.reshape(nb, Q_BLOCK, nk)

    def block(args):
        qb, ib, bb = args
        kg = kx[:, ib]
        vg = vx[:, ib]
        s_loc = jnp.einsum('bqgrd,bqkgd->bgrqk', qb, kg).astype(jnp.float32) * scale + bb
        s_ctx = jnp.einsum('bqgrd,bkgd->bgrqk', qb, kc).astype(jnp.float32) * scale
        p = jax.nn.softmax(jnp.concatenate([s_ctx, s_loc], axis=-1), axis=-1).astype(vx.dtype)
        return (jnp.einsum('bgrqk,bkgd->bqgrd', p[..., :n_ctx], vc)
                + jnp.einsum('bgrqk,bqkgd->bqgrd', p[..., n_ctx:], vg))

    out_x = merge_heads(from_blocks(lax.map(block, (to_blocks(qx), idx, bias))), w_o)
    out_c = merge_heads(attend(qc, kc, vc, scale), w_o) if need_ctx else None
    return out_x, out_c


def mixer_mla(hx, hc, need_ctx, w_dq, q_norm_g, w_uq, w_dkv, kv_norm_g, w_ukv, w_o):
    def project(h, rope):
        b, n, _ = h.shape
        cq = rms_norm(h @ w_dq, q_norm_g)
        q = (cq @ w_uq).reshape(b, n, N_HEADS, MLA_NOPE + MLA_ROPE)
        kv_a = h @ w_dkv
        ckv = rms_norm(kv_a[..., :MLA_KV_RANK], kv_norm_g)
        k_pe = kv_a[..., MLA_KV_RANK:]
        kv = (ckv @ w_ukv).reshape(b, n, N_HEADS, MLA_NOPE + MLA_V)
        k_nope, v = kv[..., :MLA_NOPE], kv[..., MLA_NOPE:]
        q_nope, q_pe = q[..., :MLA_NOPE], q[..., MLA_NOPE:]
        if rope is not None:
            cos, sin = rope
            q_pe = apply_rope(q_pe, cos, sin)
            k_pe = apply_rope(k_pe, cos, sin)
        k = jnp.concatenate([k_nope, jnp.broadcast_to(k_pe[:, :, None, :], (b, n, N_HEADS, MLA_ROPE))], axis=-1)
        q = jnp.concatenate([q_nope, q_pe], axis=-1)[:, :, :, None, :]
        return q, k, v

    n_lat = hx.shape[1]
    qx, kx, vx = project(hx, axial_rope_tables(n_lat, MLA_ROPE))
    qc, kc, vc = project(hc, None)
    scale = (MLA_NOPE + MLA_ROPE) ** -0.5
    ox = dense_blocks(qx, jnp.concatenate([kc, kx], axis=1), jnp.concatenate([vc, vx], axis=1), scale)
    out_x = merge_heads(ox, w_o)
    out_c = merge_heads(attend(qc, kc, vc, scale), w_o) if need_ctx else None
    return out_x, out_c


def conv_ffn(h, w_up, conv_w, conv_b, w_down):
    a, v = jnp.split(h @ w_up, 2, axis=-1)
    a = lax.conv_general_dilated(a, conv_w[:, None, :].astype(a.dtype), (1,),
                                 [(CONV_W // 2, CONV_W // 2)],
                                 dimension_numbers=('NWC', 'WIO', 'NWC'),
                                 feature_group_count=a.shape[-1]) + conv_b
    return (jax.nn.silu(a) * v) @ w_down


def setup_inputs(seed: int = 0) -> dict:
    key = jax.random.key(seed)
    keys = iter(jax.random.split(key, 128))

    def nrm(shape, scale):
        return jax.random.normal(next(keys), shape, jnp.float32) * scale

    def gain(n):
        return 1.0 + nrm((n,), 0.02)

    d = D_MODEL
    inp = {
        'x': nrm((BATCH, SEQ, d), 1.0),
        'c': nrm((BATCH, d), 1.0),
        'ctx': nrm((BATCH, CTX_LEN, d), 1.0),
        'c_ctx': nrm((d,), 1.0),
    }
    for i in range(DEPTH):
        p = 'l%d_' % i
        kind = i % N_MIXERS
        inp[p + 'ada_w'] = nrm((d, ADA_CHUNKS * d), 0.5 * d ** -0.5)
        inp[p + 'ada_b'] = nrm((ADA_CHUNKS * d,), 0.01)
        inp[p + 'attn_norm_g'] = gain(d)
        if kind < 3:
            inp[p + 'w_qkv'] = nrm((d, QKV_DIM), d ** -0.5)
        if kind == 0:
            inp[p + 'q_norm_g'] = gain(HEAD_DIM)
            inp[p + 'k_norm_g'] = gain(HEAD_DIM)
        elif kind == 1:
            inp[p + 'sink'] = nrm((N_HEADS,), 0.5)
        elif kind == 2:
            inp[p + 'rel_bias'] = nrm((N_HEADS, 2 * NA_KH - 1, 2 * NA_KW - 1), 0.1)
        else:
            inp[p + 'w_dq'] = nrm((d, MLA_Q_RANK), d ** -0.5)
            inp[p + 'q_norm_g'] = gain(MLA_Q_RANK)
            inp[p + 'w_uq'] = nrm((MLA_Q_RANK, N_HEADS * (MLA_NOPE + MLA_ROPE)), MLA_Q_RANK ** -0.5)
            inp[p + 'w_dkv'] = nrm((d, MLA_KV_RANK + MLA_ROPE), d ** -0.5)
            inp[p + 'kv_norm_g'] = gain(MLA_KV_RANK)
            inp[p + 'w_ukv'] = nrm((MLA_KV_RANK, N_HEADS * (MLA_NOPE + MLA_V)), MLA_KV_RANK ** -0.5)
        o_in = N_HEADS * (MLA_V if kind == 3 else HEAD_DIM)
        inp[p + 'w_o'] = nrm((o_in, d), o_in ** -0.5)
        inp[p + 'ffn_norm_g'] = gain(d)
        inp[p + 'ffn_w_up'] = nrm((d, 2 * D_FF), d ** -0.5)
        inp[p + 'ffn_conv_w'] = nrm((CONV_W, D_FF), CONV_W ** -0.5)
        inp[p + 'ffn_conv_b'] = nrm((D_FF,), 0.01)
        inp[p + 'ffn_w_down'] = nrm((D_FF, d), D_FF ** -0.5)
    inp['final_norm_g'] = gain(d)
    return inp


def reference(x, c, ctx, c_ctx,
              l0_ada_w, l0_ada_b, l0_attn_norm_g, l0_w_qkv, l0_q_norm_g, l0_k_norm_g, l0_w_o,
              l0_ffn_norm_g, l0_ffn_w_up, l0_ffn_conv_w, l0_ffn_conv_b, l0_ffn_w_down,
              l1_ada_w, l1_ada_b, l1_attn_norm_g, l1_w_qkv, l1_sink, l1_w_o,
              l1_ffn_norm_g, l1_ffn_w_up, l1_ffn_conv_w, l1_ffn_conv_b, l1_ffn_w_down,
              l2_ada_w, l2_ada_b, l2_attn_norm_g, l2_w_qkv, l2_rel_bias, l2_w_o,
              l2_ffn_norm_g, l2_ffn_w_up, l2_ffn_conv_w, l2_ffn_conv_b, l2_ffn_w_down,
              l3_ada_w, l3_ada_b, l3_attn_norm_g, l3_w_dq, l3_q_norm_g, l3_w_uq, l3_w_dkv,
              l3_kv_norm_g, l3_w_ukv, l3_w_o,
              l3_ffn_norm_g, l3_ffn_w_up, l3_ffn_conv_w, l3_ffn_conv_b, l3_ffn_w_down,
              final_norm_g):
    b, _, d = x.shape
    ada_w = [l0_ada_w, l1_ada_w, l2_ada_w, l3_ada_w]
    ada_b = [l0_ada_b, l1_ada_b, l2_ada_b, l3_ada_b]
    attn_g = [l0_attn_norm_g, l1_attn_norm_g, l2_attn_norm_g, l3_attn_norm_g]
    ffn_g = [l0_ffn_norm_g, l1_ffn_norm_g, l2_ffn_norm_g, l3_ffn_norm_g]
    ffn_p = [(l0_ffn_w_up, l0_ffn_conv_w, l0_ffn_conv_b, l0_ffn_w_down),
             (l1_ffn_w_up, l1_ffn_conv_w, l1_ffn_conv_b, l1_ffn_w_down),
             (l2_ffn_w_up, l2_ffn_conv_w, l2_ffn_conv_b, l2_ffn_w_down),
             (l3_ffn_w_up, l3_ffn_conv_w, l3_ffn_conv_b, l3_ffn_w_down)]
    mixer_p = [(l0_w_qkv, l0_q_norm_g, l0_k_norm_g, l0_w_o),
               (l1_w_qkv, l1_sink, l1_w_o),
               (l2_w_qkv, l2_rel_bias, l2_w_o),
               (l3_w_dq, l3_q_norm_g, l3_w_uq, l3_w_dkv, l3_kv_norm_g, l3_w_ukv, l3_w_o)]
    mixers = (mixer_global, mixer_window, mixer_neighbourhood, mixer_mla)

    cx = ctx
    sc = jax.nn.silu(c)
    sc_ctx = jax.nn.silu(c_ctx)
    for i in range(DEPTH):
        need_ctx = i < DEPTH - 1
        mod = (sc @ ada_w[i] + ada_b[i]).reshape(b, ADA_CHUNKS, 1, d)
        mc = (sc_ctx @ ada_w[i] + ada_b[i]).reshape(ADA_CHUNKS, d)
        hx = modulate(x, attn_g[i], mod[:, 0], mod[:, 1])
        hc = modulate(cx, attn_g[i], mc[0], mc[1])
        ax, ac = mixers[i % N_MIXERS](hx, hc, need_ctx, *mixer_p[i])
        x = x + mod[:, 2] * ax
        x = x + mod[:, 5] * conv_ffn(modulate(x, ffn_g[i], mod[:, 3], mod[:, 4]), *ffn_p[i])
        if need_ctx:
            cx = cx + mc[2] * ac
            cx = cx + mc[5] * conv_ffn(modulate(cx, ffn_g[i], mc[3], mc[4]), *ffn_p[i])
    return rms_norm(x, final_norm_g)
```

```python
import numpy as np
from contextlib import ExitStack
import concourse.bass as bass
import concourse.mybir as mybir
from concourse.bass_utils import run_bass_kernel_spmd

F32 = mybir.dt.float32
BF16 = mybir.dt.bfloat16
AF = mybir.ActivationFunctionType
ALU = mybir.AluOpType

D = 4096
DC = 32
NLAT = 4096
NCTX = 256
T = NLAT + NCTX
DFF = 6144
FC = 48
EPS = 1e-6
NCORES = 4
N_LANES = {"sp": 24, "pool": 12}
SBS = [(0, 1024, 0), (1024, 1024, 0), (2048, 1024, 0), (3072, 1024, 0), (4096, 256, 1)]


def subs_of(n, step=None):
    if step is None:
        k = -(-n // 512)
        step = -(-n // k)
    return [(s, min(step, n - s)) for s in range(0, n, step)]


class Prog:
    def __init__(self):
        self.nc = bass.Bass("TRN2", target_bir_lowering=False)
        self.es = ExitStack()
        nc = self.nc
        self.eng_names = ["pe", "act", "dve", "pool", "sp"]
        self.sem, self.cnt, self.seen, self.insts = {}, {}, {}, {}
        for n in self.eng_names:
            self.sem[n] = self.es.enter_context(nc.semaphore("c_" + n))
            self.cnt[n] = 0
            self.seen[n] = {}
            self.insts[n] = []
        self.lanes, self.lane_rr = {}, {}
        for q, nl in N_LANES.items():
            self.lanes[q] = [[self.es.enter_context(nc.semaphore("d_%s%d" % (q, i))), 0] for i in range(nl)]
            self.lane_rr[q] = 0
        self.buf = {}
        self.n_inst = 0
        self.done_sems = []
        self.nsw = 0

    def sbuf(self, name, shape, dt):
        return self.es.enter_context(self.nc.sbuf_tensor(name, list(shape), dt))

    def switch_sems(self, engs=("pe", "act", "dve")):
        for n in engs:
            self.done_sems.append((self.sem[n], self.cnt[n]))
            self.nsw += 1
            self.sem[n] = self.es.enter_context(self.nc.semaphore("c_%s_%d" % (n, self.nsw)))
            self.cnt[n] = 0

    def psum(self, name, shape, dt):
        return self.es.enter_context(self.nc.psum_tensor(name, list(shape), dt))

    def _deps(self, reads, writes):
        deps = []
        for k in reads:
            b = self.buf.get(k)
            if b and b[0] is not None:
                deps.append(b[0])
        for k in writes:
            b = self.buf.get(k)
            if b:
                if b[0] is not None:
                    deps.append(b[0])
                deps.extend(b[1])
        return deps

    def _commit(self, ev, reads, writes):
        for k in reads:
            b = self.buf.setdefault(k, [None, []])
            b[1].append(ev)
            if len(b[1]) > 48:
                mx = {}
                for s, v in b[1]:
                    if mx.get(id(s), (None, -1))[1] < v:
                        mx[id(s)] = (s, v)
                b[1] = list(mx.values())
        for k in writes:
            self.buf[k] = [ev, []]

    def _waits(self, eng, deps, same_ok):
        seen = self.seen[eng]
        ws = {}
        for s, v in deps:
            if same_ok and s is self.sem[eng]:
                continue
            if seen.get(id(s), 0) >= v:
                continue
            if ws.get(id(s), (None, 0))[1] < v:
                ws[id(s)] = (s, v)
        for s, v in ws.values():
            seen[id(s)] = v
        return list(ws.values())

    def op(self, eng, fn, reads=(), writes=(), track=True):
        deps = self._deps(reads, writes)
        waits = self._waits(eng, deps, same_ok=(eng == "pe"))
        if track:
            self.cnt[eng] += 1
            ev = (self.sem[eng], self.cnt[eng])
            inc = (self.sem[eng], 1)
        else:
            ev = (self.sem[eng], self.cnt[eng] + 1)
            inc = None
        self.insts[eng].append((waits, fn, inc))
        self._commit(ev, reads, writes)
        self.n_inst += 1
        return ev

    def dma(self, q, out, in_, reads=(), writes=(), **kw):
        lanes = self.lanes[q]
        li = self.lane_rr[q]
        self.lane_rr[q] = (li + 1) % len(lanes)
        lane = lanes[li]
        deps = self._deps(reads, writes)
        if lane[1] > 0:
            deps.append((lane[0], lane[1]))
        waits = self._waits(q, deps, same_ok=False)
        lane[1] += 16
        ev = (lane[0], lane[1])

        def fn(e, out=out, in_=in_, kw=kw):
            return e.dma_start(out=out, in_=in_, **kw)
        self.insts[q].append((waits, fn, (lane[0], 16)))
        self._commit(ev, reads, writes)
        self.n_inst += 1
        return ev

    def finish(self):
        nc = self.nc
        final = []
        for q in self.lanes:
            for s, v in self.lanes[q]:
                if v > 0:
                    final.append((s, v))
        for n in self.eng_names:
            if self.cnt[n] > 0:
                final.append((self.sem[n], self.cnt[n]))
        final.extend([(s, v) for (s, v) in self.done_sems if v > 0])
        insts = self.insts

        def run(e, lst, extra=()):
            for waits, fn, inc in lst:
                for s, v in waits:
                    e.wait_ge(s, v)
                ins = fn(e)
                if inc is not None:
                    ins.then_inc(inc[0], inc[1])
            for s, v in extra:
                e.wait_ge(s, v)

        with nc.Block() as block:
            @block.tensor
            def _(e):
                run(e, insts["pe"])

            @block.scalar
            def _(e):
                run(e, insts["act"])

            @block.vector
            def _(e):
                run(e, insts["dve"])

            @block.gpsimd
            def _(e):
                run(e, insts["pool"])

            @block.sync
            def _(e):
                run(e, insts["sp"], extra=final)
        return nc


class Ring:
    def __init__(self, name, tiles, keys=None):
        self.name, self.tiles, self.i = name, tiles, 0
        self.keys = keys if keys is not None else [(name, k) for k in range(len(tiles))]

    def next(self):
        k = self.i % len(self.tiles)
        self.i += 1
        return self.tiles[k], self.keys[k]


class Builder:
    def __init__(self, n_layers=4, debug_out=None, stop_after=None):
        self.stop_after = stop_after
        self.P = Prog()
        self.n_layers = n_layers
        self.debug_out = debug_out
        self.inp = {}
        self.alloc()

    def din(self, name, shape, dt=F32):
        t = self.P.nc.dram_tensor(name, list(shape), dt, kind="ExternalInput").ap()
        self.inp[name] = t
        return t

    def dscr(self, name, shape, dt):
        return self.P.nc.dram_tensor(name, list(shape), dt, kind="Internal").ap()

    def alloc(self):
        P = self.P
        self.xT = self.din("xT", [D, T])
        self.cvec = self.din("cvec", [128, DC, 2])
        self.ropeC = self.din("ropeC", [128, T])
        self.ropeS = self.din("ropeS", [128, T])
        self.ropeC64 = self.din("ropeC64", [64, T])
        self.ropeS64 = self.din("ropeS64", [64, T])
        self.rotm = self.din("rotm", [128, 128])
        self.rotm64 = self.din("rotm64", [64, 64])
        self.wmask = self.din("wmask", [6, 128, 512])
        self.nmask = self.din("nmask", [3, 8, 128, 512])
        self.fin_g = self.din("fin_g", [128, DC])
        self.L = []
        for l in range(self.n_layers):
            p = "l%d_" % l
            d = {}
            d["ada_w"] = self.din(p + "ada_w", [D, 6 * D])
            d["ada_b"] = self.din(p + "ada_b", [128, 192])
            d["attn_g"] = self.din(p + "attn_g", [128, DC])
            d["ffn_g"] = self.din(p + "ffn_g", [128, DC])
            if l < 3:
                d["w_qkv"] = self.din(p + "w_qkv", [D, 6144])
            if l == 0:
                d["qk_g"] = self.din(p + "qk_g", [128, 2])
            if l == 1:
                d["sink"] = self.din(p + "sink", [128, 32])
            if l == 2:
                d["relb"] = self.din(p + "relb", [32, 8, 128, 512])
            if l == 3:
                d["w_dq"] = self.din(p + "w_dq", [D, 1024])
                d["q_g"] = self.din(p + "q_g", [128, 8])
                d["w_uq"] = self.din(p + "w_uq", [1024, 6144])
                d["w_dkv"] = self.din(p + "w_dkv", [D, 576])
                d["kv_g"] = self.din(p + "kv_g", [128, 4])
                d["w_ukv"] = self.din(p + "w_ukv", [512, 8192])
            d["w_o"] = self.din(p + "w_o", [D, D])
            d["w_up"] = self.din(p + "w_up", [D, 2 * DFF])
            d["conv_w"] = self.din(p + "conv_w", [128, FC, 3])
            d["conv_b"] = self.din(p + "conv_b", [128, FC])
            d["w_down"] = self.din(p + "w_down", [DFF, D])
            self.L.append(d)
        self.out = P.nc.dram_tensor("outT", [D, NLAT], F32, kind="ExternalOutput").ap()
        self.xs = self.dscr("xs", [D, T], F32)
        self.xm = self.dscr("xm", [D, T], F32)
        self.qT = self.dscr("qT", [D, T], BF16)
        self.qpT = self.dscr("qpT", [2048, T], BF16)
        self.kT = self.dscr("kT", [D + 64, T], BF16)
        self.vtm = self.dscr("vtm", [T, D], BF16)
        self.oT = self.dscr("oT", [D, T], BF16)
        self.uT = self.dscr("uT", [DFF, T], BF16)
        self.abuf = P.sbuf("abuf", [128, 33792], BF16)
        self.wbuf = Ring("w", [P.sbuf("wbuf%d" % i, [128, 8192], BF16) for i in range(4)])
        self.xr = Ring("x", [P.sbuf("xt%d" % i, [128, 512], F32) for i in range(4)])
        self.fr = Ring("f", [P.sbuf("ft%d" % i, [128, 512], F32) for i in range(8)])
        self.br = Ring("b", [P.sbuf("bt%d" % i, [128, 512], BF16) for i in range(4)])
        self.rstd = P.sbuf("rstd", [128, 512], F32)
        self.ones_f = P.sbuf("ones_f", [128, 128], F32)
        self.ones_b = P.sbuf("ones_b", [128, 128], BF16)
        self.rot_sb = P.sbuf("rot_sb", [128, 128], F32)
        self.rot64_sb = P.sbuf("rot64_sb", [64, 64], F32)
        self.cos_sb = P.sbuf("cos_sb", [128, 1024], F32)
        self.sin_sb = P.sbuf("sin_sb", [128, 1024], F32)
        self.scT = P.sbuf("scT", [128, DC, 2], BF16)
        self.cv_sb = P.sbuf("cv_sb", [128, DC, 2], F32)
        self.modv = [P.sbuf("modv%d" % l, [128, 192, 2], F32) for l in range(4)]
        self.A1 = [P.sbuf("A1_%d" % l, [128, DC, 2], F32) for l in range(4)]
        self.A2 = [P.sbuf("A2_%d" % l, [128, DC, 2], F32) for l in range(4)]
        self.small = P.sbuf("small", [128, 256], F32)
        self.cw = P.sbuf("cw", [128, FC, 3], F32)
        self.cb = P.sbuf("cb", [128, FC], F32)
        self.g_sb = P.sbuf("g_sb", [128, DC], F32)
        self.eps_sb = P.sbuf("eps_sb", [128, 1], F32)
        self.dummy = P.sbuf("bar_dmy", [128, 8], F32)
        self.apad = P.sbuf("apad", [128, 1028], F32)
        self.vpad = P.sbuf("vpad", [128, 1028], F32)
        self.eps_ap = self.eps_sb[:, 0:1]
        if self.n_layers > 3:
            self.mbuf = P.sbuf("mbuf", [128, 8, 256], F32)
            self.cqn = P.sbuf("cqn", [128, 8, 256], BF16)
            self.ckvn = P.sbuf("ckvn", [128, 4, 256], BF16)
        self.ps = [P.psum("ps%d" % i, [128, 512], F32) for i in range(8)]
        self.pk = [("psb", i) for i in range(8)]
        self.psr = Ring("ps", self.ps[0:4], self.pk[0:4])
        self.nbar = 0
        self.mod_state = {}
        self.bg = None

    def barrier(self, eng, keys, reads=()):
        i = self.nbar % 8
        self.nbar += 1
        if eng == "act":
            fn = lambda e, i=i: e.activation(out=self.dummy[:, i:i + 1], in_=self.eps_sb[:, 0:1], func=AF.Copy)
        else:
            fn = lambda e, i=i: e.memset(self.dummy[:, i:i + 1], 0.0)
        return self.P.op(eng, fn, reads=list(reads), writes=list(keys))

    def setup_consts(self):
        P = self.P
        P.op("pool", lambda e: e.memset(self.ones_f[:], 1.0), writes=["ones_f"])
        P.op("pool", lambda e: e.memset(self.ones_b[:], 1.0), writes=["ones_b"])
        P.op("pool", lambda e: e.memset(self.eps_sb[:], EPS), writes=["eps"])
        P.dma("sp", self.rot_sb[:], self.rotm, writes=["rot"])
        P.dma("sp", self.rot64_sb[:], self.rotm64, writes=["rot64"])
        P.dma("sp", self.cv_sb[:], self.cvec, writes=["cv"])
        P.op("act", lambda e: e.activation(out=self.scT[:], in_=self.cv_sb[:], func=AF.Silu),
             reads=["cv"], writes=["scT"])

    def mod_slabs(self, l, count):
        P = self.P
        Lw = self.L[l]
        adaw = Lw["ada_w"].rearrange("(c p) n -> p c n", p=128)
        ps = self.ps[7]
        st = self.mod_state.setdefault(l, {"next": 0})
        for _ in range(count):
            slab = st["next"]
            if slab >= 96:
                return
            st["next"] += 1
            wb, wk = self.wbuf.next()
            wv = wb[:, 0:DC * 256].rearrange("p (c n) -> p c n", n=256)
            P.dma("pool", wv, adaw[:, :, slab * 256:(slab + 1) * 256], writes=[wk])
            for mi in range(2):
                m = slab * 2 + mi
                for k in range(DC):
                    P.op("pe", lambda e, wv=wv, mi=mi, k=k, m=m: e.matmul(
                        ps[:, 2 * m:2 * m + 2], lhsT=wv[:, k, mi * 128:(mi + 1) * 128], rhs=self.scT[:, k, :],
                        start=(k == 0), stop=(k == DC - 1)),
                        reads=[wk, "scT"], writes=[self.pk[7]], track=(k == DC - 1))

    def phase_mod(self, l):
        P = self.P
        Lw = self.L[l]
        ps = self.ps[7]
        self.mod_slabs(l, 96)
        mv = self.modv[l]
        ab = self.small
        P.dma("sp", ab[:, 0:192], Lw["ada_b"], writes=["small"])
        psv = ps[:, 0:384].rearrange("p (m j) -> p m j", j=2)
        for col in range(2):
            P.op("dve", lambda e, col=col: e.tensor_tensor(out=mv[:, :, col], in0=psv[:, :, col], in1=ab[:, 0:192], op=ALU.add),
                 reads=[self.pk[7], "small"], writes=[("modv", l)])
        for (gname, dst, base) in (("attn_g", self.A1[l], 32), ("ffn_g", self.A2[l], 128)):
            P.dma("sp", self.g_sb[:], Lw[gname], writes=["g_sb"])
            for col in range(2):
                P.op("dve", lambda e, col=col, dst=dst, base=base: e.scalar_tensor_tensor(
                    out=dst[:, :, col], in0=mv[:, base:base + DC, col], scalar=1.0, in1=self.g_sb[:],
                    op0=ALU.add, op1=ALU.mult), reads=[("modv", l), "g_sb"], writes=[("A", l, base)])

    def mods(self, l):
        mv = self.modv[l]
        return dict(A1=self.A1[l], B1=mv[:, 0:32, :], G1=mv[:, 64:96, :],
                    A2=self.A2[l], B2=mv[:, 96:128, :], G2=mv[:, 160:192, :])

    def norm_block(self, src, src_keys, cols, A, B, col, dst, modkeys):
        P = self.P
        t0, n = cols
        ps = self.ps[6]
        self.barrier("act", ["abuf"])
        for (s, w) in subs_of(n):
            for j in range(DC):
                xt, xk = self.xr.next()
                P.dma("sp", xt[:, :w], src[j * 128:(j + 1) * 128, t0 + s:t0 + s + w], reads=src_keys, writes=[xk])
                ft, fk = self.fr.next()
                P.op("act", lambda e, ft=ft, xt=xt, w=w: e.activation(out=ft[:, :w], in_=xt[:, :w], func=AF.Square),
                     reads=[xk], writes=[fk])
                P.op("pe", lambda e, ft=ft, w=w, j=j: e.matmul(ps[:, :w], lhsT=self.ones_f[:], rhs=ft[:, :w],
                                                             start=(j == 0), stop=(j == DC - 1)),
                     reads=[fk, "ones_f"], writes=[self.pk[6]])
            ft, fk = self.fr.next()
            P.op("act", lambda e, ft=ft, w=w: e.activation(out=ft[:, :w], in_=ps[:, :w], func=AF.Sqrt,
                                                         scale=1.0 / D, bias=self.eps_ap),
                 reads=[self.pk[6], "eps"], writes=[fk])
            P.op("dve", lambda e, ft=ft, w=w: e.reciprocal(out=self.rstd[:, :w], in_=ft[:, :w]),
                 reads=[fk], writes=["rstd"])
            for j in range(DC):
                xt, xk = self.xr.next()
                P.dma("sp", xt[:, :w], src[j * 128:(j + 1) * 128, t0 + s:t0 + s + w], reads=src_keys, writes=[xk])
                ft, fk = self.fr.next()
                P.op("dve", lambda e, ft=ft, xt=xt, w=w: e.tensor_tensor(out=ft[:, :w], in0=xt[:, :w], in1=self.rstd[:, :w], op=ALU.mult),
                     reads=[xk, "rstd"], writes=[fk])
                P.op("act", lambda e, ft=ft, w=w, j=j, s=s: e.activation(
                    out=dst[:, j, s:s + w], in_=ft[:, :w], func=AF.Identity,
                    scale=A[:, j, col:col + 1], bias=B[:, j, col:col + 1]),
                    reads=[fk, "abuf"] + modkeys)
        self.barrier("act", ["abuf"])

    def gemm_fm(self, w, KC, slabs, act, act_key, subs, epi):
        P = self.P
        wv_d = w.rearrange("(c p) n -> p c n", p=128)
        kparts = [(0, KC)] if KC <= 32 else [(0, KC // 2), (KC // 2, KC - KC // 2)]
        pend = []
        for (c0, ncols, chunks) in slabs:
            wvs, wks = [], []
            for (k0, kn) in kparts:
                wb, wk = self.wbuf.next()
                wv = wb[:, 0:kn * ncols].rearrange("p (c n) -> p c n", n=ncols)
                P.dma("pool", wv, wv_d[:, k0:k0 + kn, c0:c0 + ncols], writes=[wk])
                wvs.append(wv)
                wks.append(wk)
            for (off, M) in chunks:
                for (s, wd) in subs:
                    pt, pk = self.psr.next()
                    for k in range(KC):
                        pi = 0 if k < kparts[0][1] else 1
                        kk = k - kparts[pi][0]
                        P.op("pe", lambda e, pt=pt, wv=wvs[pi], kk=kk, off=off, M=M, k=k, s=s, wd=wd: e.matmul(
                            pt[0:M, :wd], lhsT=wv[:, kk, off:off + M], rhs=act[:, k, s:s + wd],
                            start=(k == 0), stop=(k == KC - 1)),
                            reads=[wks[pi], act_key], writes=[pk], track=(k == KC - 1))
                    for gen in list(pend):
                        try:
                            next(gen)
                        except StopIteration:
                            pend.remove(gen)
                    r = epi(c0 + off, M, s, wd, pt, pk)
                    if r is not None:
                        try:
                            next(r)
                            pend.append(r)
                        except StopIteration:
                            pass
        while pend:
            for gen in list(pend):
                try:
                    next(gen)
                except StopIteration:
                    pend.remove(gen)

    def gemm_tm(self, w, KC, c0, ncols, act, act_key, ntiles, epi):
        P = self.P
        wv_d = w.rearrange("(c p) n -> p c n", p=128)
        for g0 in range(0, ncols, 256):
            wb, wk = self.wbuf.next()
            wv = wb[:, 0:KC * 256].rearrange("p (c n) -> p c n", n=256)
            P.dma("pool", wv, wv_d[:, :, c0 + g0:c0 + g0 + 256], writes=[wk])
            for tt in range(ntiles):
                pt, pk = self.psr.next()
                for k in range(KC):
                    P.op("pe", lambda e, pt=pt, wv=wv, k=k, tt=tt: e.matmul(
                        pt[:, 0:256], lhsT=act[:, k, tt * 128:(tt + 1) * 128], rhs=wv[:, k, :],
                        start=(k == 0), stop=(k == KC - 1)),
                        reads=[wk, act_key], writes=[pk], track=(k == KC - 1))
                epi(g0, tt, pt, pk)

    def qk_epilogue(self, *a, **kw):
        for _ in self.qk_epi_gen(*a, **kw):
            pass

    def qk_epi_gen(self, l, do_norm, do_rope, t0, M, s, wd, pt, pk, gcol, dst, drow, wkey, cs=None):
        cs = s if cs is None else cs
        P = self.P
        if not do_norm and not do_rope:
            bt, bk = self.br.next()
            P.op("act", lambda e: e.activation(out=bt[0:M, :wd], in_=pt[0:M, :wd], func=AF.Copy), reads=[pk], writes=[bk])
            P.dma("sp", dst[drow:drow + M, t0 + s:t0 + s + wd], bt[0:M, :wd], reads=[bk, wkey])
            return
        qf, qfk = self.fr.next()
        if do_norm:
            sq, sqk = self.fr.next()
            P.op("act", lambda e: e.activation(out=sq[:, :wd], in_=pt[:, :wd], func=AF.Square), reads=[pk], writes=[sqk])
            p2 = self.ps[4]
            yield
            P.op("pe", lambda e: e.matmul(p2[:, :wd], lhsT=self.ones_f[:], rhs=sq[:, :wd], start=True, stop=True),
                 reads=[sqk, "ones_f"], writes=[self.pk[4]])
            P.op("act", lambda e: e.activation(out=sq[:, :wd], in_=p2[:, :wd], func=AF.Sqrt, scale=1.0 / 128, bias=self.eps_ap),
                 reads=[self.pk[4], "eps"], writes=[sqk])
            P.op("dve", lambda e: e.reciprocal(out=sq[:, :wd], in_=sq[:, :wd]), reads=[sqk], writes=[sqk])
            P.op("dve", lambda e: e.scalar_tensor_tensor(out=qf[:, :wd], in0=pt[:, :wd], scalar=self.qkg[:, gcol:gcol + 1],
                                                         in1=sq[:, :wd], op0=ALU.mult, op1=ALU.mult),
                 reads=[pk, sqk, "qkg"], writes=[qfk])
        else:
            P.op("act", lambda e: e.activation(out=qf[0:M, :wd], in_=pt[0:M, :wd], func=AF.Copy), reads=[pk], writes=[qfk])
        bt, bk = self.br.next()
        if do_rope:
            p3 = self.ps[5]
            rot = self.rot_sb if M == 128 else self.rot64_sb
            yield
            P.op("pe", lambda e: e.matmul(p3[0:M, :wd], lhsT=rot[:], rhs=qf[0:M, :wd], start=True, stop=True),
                 reads=[qfk, "rot", "rot64"], writes=[self.pk[5]])
            t1, t1k = self.fr.next()
            P.op("dve", lambda e: e.tensor_tensor(out=t1[0:M, :wd], in0=p3[0:M, :wd], in1=self.sin_sb[0:M, cs:cs + wd], op=ALU.mult),
                 reads=[self.pk[5], "sin"], writes=[t1k])
            P.op("dve", lambda e: e.tensor_tensor(out=qf[0:M, :wd], in0=qf[0:M, :wd], in1=self.cos_sb[0:M, cs:cs + wd], op=ALU.mult),
                 reads=[qfk, "cos"], writes=[qfk])
            P.op("dve", lambda e: e.tensor_tensor(out=bt[0:M, :wd], in0=qf[0:M, :wd], in1=t1[0:M, :wd], op=ALU.add),
                 reads=[qfk, t1k], writes=[bk])
        else:
            P.op("act", lambda e: e.activation(out=bt[0:M, :wd], in_=qf[0:M, :wd], func=AF.Copy), reads=[qfk], writes=[bk])
        P.dma("sp", dst[drow:drow + M, t0 + s:t0 + s + wd], bt[0:M, :wd], reads=[bk, wkey])

    def load_rope(self, t0, n, half=False):
        P = self.P
        if half:
            P.dma("sp", self.cos_sb[0:64, :n], self.ropeC64[:, t0:t0 + n], writes=["cos"])
            P.dma("sp", self.sin_sb[0:64, :n], self.ropeS64[:, t0:t0 + n], writes=["sin"])
        else:
            P.dma("sp", self.cos_sb[:, :n], self.ropeC[:, t0:t0 + n], writes=["cos"])
            P.dma("sp", self.sin_sb[:, :n], self.ropeS[:, t0:t0 + n], writes=["sin"])

    def stage1_gqa(self, l, kind, src, src_key):
        P = self.P
        Lw = self.L[l]
        md = self.mods(l)
        wkey = ("qkvW", l)
        if kind == 0:
            P.dma("sp", self.small[:, 200:202], Lw["qk_g"], writes=["qkg"])
            self.qkg = self.small[:, 200:202]
        slabs = [(c0, 256, [(0, 128), (128, 128)]) for c0 in range(0, 5120, 256)]
        for sbi, (t0, n, col) in enumerate(SBS):
            hT = self.abuf[:, 0:DC * n].rearrange("p (c t) -> p c t", t=n)
            self.norm_block(src, [(src_key, sbi)], (t0, n), md["A1"], md["B1"], col, hT,
                            [("modv", l), ("A", l, 32)])
            is_ctx = (col == 1)
            do_norm = (kind == 0)
            do_rope = (kind in (0, 1)) and not is_ctx
            if do_rope:
                self.load_rope(t0, n)

            def epi(c, M, s_, wd_, pt, pk, t0=t0, do_norm=do_norm, do_rope=do_rope):
                if c < D:
                    return self.qk_epi_gen(l, do_norm, do_rope, t0, M, s_, wd_, pt, pk, 0, self.qT, c, wkey)
                return self.qk_epi_gen(l, do_norm, do_rope, t0, M, s_, wd_, pt, pk, 1, self.kT, c - D, wkey)
            self.gemm_fm(Lw["w_qkv"], DC, slabs, hT, "abuf", subs_of(n), epi)

            def epi_v(g0, tt, pt, pk, t0=t0):
                bt, bk = self.br.next()
                P.op("act", lambda e: e.activation(out=bt[:, 0:256], in_=pt[:, 0:256], func=AF.Copy), reads=[pk], writes=[bk])
                P.dma("sp", self.vtm[t0 + tt * 128:t0 + (tt + 1) * 128, g0:g0 + 256], bt[:, 0:256], reads=[bk, wkey])
            self.gemm_tm(Lw["w_qkv"], DC, 5120, 1024, hT, "abuf", n // 128, epi_v)
        self.barrier("dve", [wkey])

    def attention(self, l, n_groups, q_per, key_tiles_fn, scale, need_ctx, mla=False, mask_fn=None, sink=False,
                  kv_of=None, group_hook=None):
        P = self.P
        A = self.abuf
        rk = ("qkvW", l)
        wkey = ("oW", l)
        KT = A[:, 0:T]
        VT = A[:, 4352:4352 + 34 * 128].rearrange("p (k d) -> p k d", d=128)
        QT = [A[:, 8704 + i * 2048:8704 + (i + 1) * 2048].rearrange("p (r t) -> p r t", t=512) for i in range(2)]
        KP = A[0:64, 12800:12800 + T]
        QP = [A[0:64, 17152 + i * 512:17152 + (i + 1) * 512] for i in range(2)]
        ptr = Ring("pt", [A[:, 18176 + i * 512:18176 + (i + 1) * 512] for i in range(4)])
        pss = Ring("pss", [self.ps[0], self.ps[1], self.ps[6]], [self.pk[0], self.pk[1], self.pk[6]])
        pso = Ring("pso", self.ps[2:4], self.pk[2:4])
        psd = Ring("psd", self.ps[4:6], self.pk[4:6])
        akeys = ["KT", "VT", ("QT", 0), ("QT", 1), "KP"] + ptr.keys
        self.barrier("dve", ["abuf"] + akeys)
        qblocks = [(t0, 512, 0) for t0 in range(0, NLAT, 512)]
        if need_ctx:
            qblocks.append((NLAT, NCTX, 1))
        if mla:
            P.dma("sp", KP, self.kT[D:D + 64, :], reads=[rk], writes=["KP"])
        work = [(g, qb) for g in range(n_groups) for qb in qblocks]

        def load_q(wi):
            g, (q0, qn, is_ctx) = work[wi]
            qt, qk = QT[wi % 2], ("QT", wi % 2)
            h0 = g * q_per
            P.dma("sp", qt[:, 0:q_per, 0:qn],
                  self.qT[h0 * 128:(h0 + q_per) * 128, q0:q0 + qn].rearrange("(r p) t -> p r t", p=128),
                  reads=[rk], writes=[qk])
            if mla:
                P.dma("sp", QP[wi % 2][:, 0:qn], self.qpT[g * 64:(g + 1) * 64, q0:q0 + qn], reads=[rk], writes=[qk])
        load_q(0)
        LOOK = 2
        jobs = []
        for wi, (g, (q0, qn, is_ctx)) in enumerate(work):
            ktl = key_tiles_fn(q0, is_ctx)
            for r in range(q_per):
                for i, kt in enumerate(ktl):
                    jobs.append((wi, g, r, i, kt, len(ktl)))
        kvf = (lambda g: g) if kv_of is None else kv_of
        nj = len(jobs)
        job_ps = {}
        state = {"qk_next": 0}

        def emit_qk(j):
            wi, g, r, i, kt, n = jobs[j]
            q0, qn, is_ctx = work[wi][1]
            qt, qk, qp = QT[wi % 2], ("QT", wi % 2), QP[wi % 2]
            psx, psk = pss.next()
            job_ps[j] = (psx, psk)
            P.op("pe", lambda e, psx=psx, kt=kt, qt=qt, r=r, qn=qn: e.matmul(
                psx[:, :qn], lhsT=KT[:, kt * 128:(kt + 1) * 128], rhs=qt[:, r, 0:qn], start=True, stop=(not mla)),
                reads=["KT", qk], writes=[psk], track=(not mla))
            if mla:
                P.op("pe", lambda e, psx=psx, kt=kt, qp=qp, qn=qn: e.matmul(
                    psx[:, :qn], lhsT=KP[:, kt * 128:(kt + 1) * 128], rhs=qp[:, 0:qn], start=False, stop=True),
                    reads=["KP", qk], writes=[psk])

        cur = {}
        for j in range(nj):
            wi, g, r, i, kt, n = jobs[j]
            q0, qn, is_ctx = work[wi][1]
            kvi = kvf(g)
            hh = g * q_per + r
            if j == 0 or jobs[j - 1][0] != wi:
                new_run = (j == 0) or (kvf(jobs[j - 1][1]) != kvi)
                if new_run:
                    assert state["qk_next"] == j
                    P.dma("sp", KT, self.kT[kvi * 128:(kvi + 1) * 128, :], reads=[rk], writes=["KT"])
                    vcol = kvi * 128
                    for kt0 in range(0, 34, 17):
                        P.dma("sp", VT[:, kt0:kt0 + 17, :],
                              self.vtm[kt0 * 128:(kt0 + 17) * 128, vcol:vcol + 128].rearrange("(k p) d -> p k d", p=128),
                              reads=[rk], writes=["VT"])
                if (j == 0 or jobs[j - 1][1] != g) and group_hook is not None:
                    group_hook(g)
                if wi + 1 < len(work):
                    load_q(wi + 1)
                if self.bg is not None:
                    self.bg()
            while (state["qk_next"] < nj and state["qk_next"] <= j + LOOK
                   and kvf(jobs[state["qk_next"]][1]) == kvi):
                emit_qk(state["qk_next"])
                state["qk_next"] += 1
            if i == 0:
                cur["po"], cur["pok"] = pso.next()
                cur["pd"], cur["pdk"] = psd.next()
            po, pok, pd, pdk = cur["po"], cur["pok"], cur["pd"], cur["pdk"]
            psx, psk = job_ps.pop(j)
            pt, ptk = ptr.next()
            P.op("act", lambda e, pt=pt, psx=psx, qn=qn: e.activation(out=pt[:, :qn], in_=psx[:, :qn], func=AF.Exp, scale=scale),
                 reads=[psk], writes=[ptk])
            if mask_fn is not None:
                for (mk_ap, mk_key) in mask_fn(hh, q0, kt, is_ctx):
                    P.op("dve", lambda e, pt=pt, mk_ap=mk_ap, qn=qn: e.tensor_tensor(out=pt[:, :qn], in0=pt[:, :qn], in1=mk_ap[:, :qn], op=ALU.mult),
                         reads=[ptk, mk_key], writes=[ptk])
            last = (i == n - 1)
            P.op("pe", lambda e, po=po, pt=pt, kt=kt, qn=qn, i=i, last=last: e.matmul(
                po[:, :qn], lhsT=VT[:, kt, :], rhs=pt[:, :qn], start=(i == 0), stop=last),
                reads=["VT", ptk], writes=[pok], track=False)
            P.op("pe", lambda e, pd=pd, pt=pt, qn=qn, i=i, last=last: e.matmul(
                pd[:, :qn], lhsT=self.ones_b[:], rhs=pt[:, :qn], start=(i == 0), stop=last),
                reads=["ones_b", ptk], writes=[pdk], track=True)
            if last:
                ft, fk = self.fr.next()
                if sink:
                    P.op("dve", lambda e, ft=ft, pd=pd, qn=qn, hh=hh: e.tensor_scalar(
                        out=ft[:, :qn], in0=pd[:, :qn], scalar1=self.esink[:, hh:hh + 1], scalar2=None, op0=ALU.add),
                        reads=[pdk, "esink"], writes=[fk])
                    P.op("dve", lambda e, ft=ft, qn=qn: e.reciprocal(out=ft[:, :qn], in_=ft[:, :qn]), reads=[fk], writes=[fk])
                else:
                    P.op("dve", lambda e, ft=ft, pd=pd, qn=qn: e.reciprocal(out=ft[:, :qn], in_=pd[:, :qn]), reads=[pdk], writes=[fk])
                bt, bk = self.br.next()
                P.op("dve", lambda e, bt=bt, po=po, ft=ft, qn=qn: e.tensor_tensor(out=bt[:, :qn], in0=po[:, :qn], in1=ft[:, :qn], op=ALU.mult),
                     reads=[pok, fk], writes=[bk])
                P.dma("sp", self.oT[hh * 128:(hh + 1) * 128, q0:q0 + qn], bt[:, :qn], reads=[bk, wkey])
        self.barrier("dve", [wkey, "abuf"] + akeys)


    def attention_window(self, l, need_ctx):
        P = self.P
        Lw = self.L[l]
        A = self.abuf
        wm = A[:, 20480:20480 + 6 * 512].rearrange("p (i q) -> p i q", q=512)
        self.barrier("dve", ["abuf", "wm"])
        P.dma("pool", wm, self.wmask.rearrange("i p q -> p i q"), writes=["wm"])
        P.dma("sp", self.small[:, 210:242], Lw["sink"], writes=["esink"])
        self.esink = self.small[:, 210:242]
        P.op("act", lambda e: e.activation(out=self.small[:, 210:242], in_=self.small[:, 210:242], func=AF.Exp),
             reads=["esink"], writes=["esink"])

        def ktiles(q0, is_ctx):
            if is_ctx:
                return [32, 33]
            b = q0 // 128
            return [32, 33] + [kt for kt in range(b - 1, b + 5) if 0 <= kt < 32]

        def mask_fn(hh, q0, kt, is_ctx):
            if is_ctx or kt >= 32:
                return []
            di = kt - q0 // 128 + 1
            return [(wm[:, di, :], "wm")]
        self.attention(l, 8, 4, ktiles, 128 ** -0.5, need_ctx, mask_fn=mask_fn, sink=True)
        self.barrier("dve", ["wm", "abuf"])

    def attention_nbr(self, l, need_ctx):
        P = self.P
        Lw = self.L[l]
        A = self.abuf
        EM = A[:, 20480:20480 + 24 * 512].rearrange("p (i q) -> p i q", q=512)
        self.barrier("dve", ["abuf", "EM"])

        def hook(h):
            for kr in range(8):
                ft, fk = self.fr.next()
                P.dma("sp", ft[:, :], Lw["relb"][h, kr], writes=[fk])
                P.op("act", lambda e, ft=ft: e.activation(out=ft[:, :], in_=ft[:, :], func=AF.Exp), reads=[fk], writes=[fk])
                for ty in range(3):
                    f2, f2k = self.fr.next()
                    P.dma("sp", f2[:, :], self.nmask[ty, kr], writes=[f2k])
                    P.op("dve", lambda e, ft=ft, f2=f2, ty=ty, kr=kr: e.tensor_tensor(
                        out=EM[:, ty * 8 + kr, :], in0=ft[:, :], in1=f2[:, :], op=ALU.mult),
                        reads=[fk, f2k], writes=["EM"])

        def ktiles(q0, is_ctx):
            if is_ctx:
                return [32, 33]
            j = q0 // 512
            return [32, 33] + [kt for kt in range(4 * j - 2, 4 * j + 6) if 0 <= kt < 32]

        def mask_fn(hh, q0, kt, is_ctx):
            if is_ctx or kt >= 32:
                return []
            j = q0 // 512
            ty = 0 if j == 0 else (2 if j == 7 else 1)
            kr = kt - (4 * j - 2)
            return [(EM[:, ty * 8 + kr, :], "EM")]
        self.attention(l, 32, 1, ktiles, 128 ** -0.5, need_ctx, mask_fn=mask_fn, kv_of=lambda g: g // 4, group_hook=hook)
        self.barrier("dve", ["EM", "abuf"])

    def stage1_mla(self, l, src, src_key):
        P = self.P
        Lw = self.L[l]
        md = self.mods(l)
        wkey = ("qkvW", l)
        P.dma("sp", self.small[:, 200:208], Lw["q_g"], writes=["mlag"])
        P.dma("sp", self.small[:, 208:212], Lw["kv_g"], writes=["mlag"])
        qg = self.small[:, 200:208]
        kvg = self.small[:, 208:212]
        mb = self.mbuf
        ps6 = self.ps[6]
        for sbi, (t0, n, col) in enumerate(SBS):
            hT = self.abuf[:, 0:DC * n].rearrange("p (c t) -> p c t", t=n)
            self.norm_block(src, [(src_key, sbi)], (t0, n), md["A1"], md["B1"], col, hT, [("modv", l), ("A", l, 32)])
            is_ctx = (col == 1)
            do_rope = not is_ctx
            if do_rope:
                self.load_rope(t0, n, half=True)
            for (s0, wd0) in subs_of(n, 256):
                cqn = self.cqn[:, :, 0:wd0]
                ckvn = self.ckvn[:, :, 0:wd0]

                def low_rank(w, ncol, nch, gtile, dstn, dkey, rope_tail, s0=s0, wd0=wd0, t0=t0, do_rope=do_rope, hT=hT):
                    def epi(c, M, s_, wd_, pt, pk):
                        j = c // 128
                        if c >= ncol:
                            self.qk_epilogue(l, False, do_rope, t0, M, s0, wd0, pt, pk, 0, self.kT, D, wkey, cs=s0)
                            return
                        P.op("act", lambda e: e.activation(out=mb[:, j, 0:wd0], in_=pt[:, 0:wd0], func=AF.Copy), reads=[pk], writes=[("mb", j)])
                        ft, fk = self.fr.next()
                        P.op("act", lambda e: e.activation(out=ft[:, 0:wd0], in_=pt[:, 0:wd0], func=AF.Square), reads=[pk], writes=[fk])
                        P.op("pe", lambda e: e.matmul(ps6[:, 0:wd0], lhsT=self.ones_f[:], rhs=ft[:, 0:wd0], start=(j == 0), stop=(j == nch - 1)),
                             reads=[fk, "ones_f"], writes=[self.pk[6]])
                    slabs = [(c0, 256, [(0, 128), (128, 128)]) for c0 in range(0, ncol, 256)]
                    if rope_tail:
                        slabs.append((ncol, 64, [(0, 64)]))
                    self.gemm_fm(w, DC, slabs, hT, "abuf", [(s0, wd0)], epi)
                    ft, fk = self.fr.next()
                    P.op("act", lambda e: e.activation(out=ft[:, 0:wd0], in_=ps6[:, 0:wd0], func=AF.Sqrt, scale=1.0 / ncol, bias=self.eps_ap),
                         reads=[self.pk[6], "eps"], writes=[fk])
                    P.op("dve", lambda e: e.reciprocal(out=self.rstd[:, 0:wd0], in_=ft[:, 0:wd0]), reads=[fk], writes=["rstd"])
                    for j in range(nch):
                        P.op("dve", lambda e, j=j: e.scalar_tensor_tensor(out=dstn[:, j, 0:wd0], in0=mb[:, j, 0:wd0], scalar=gtile[:, j:j + 1],
                                                                     in1=self.rstd[:, 0:wd0], op0=ALU.mult, op1=ALU.mult),
                             reads=[("mb", j), "rstd", "mlag"], writes=[dkey])
                low_rank(Lw["w_dq"], 1024, 8, qg, cqn, "cqn", False)
                low_rank(Lw["w_dkv"], 512, 4, kvg, ckvn, "ckvn", True)

                def epi_q(c, M, s_, wd_, pt, pk, s0=s0, wd0=wd0, t0=t0, do_rope=do_rope):
                    h = c // 192
                    if M == 128:
                        self.qk_epilogue(l, False, False, t0, M, s0, wd0, pt, pk, 0, self.qT, h * 128, wkey, cs=s0)
                    else:
                        self.qk_epilogue(l, False, do_rope, t0, M, s0, wd0, pt, pk, 0, self.qpT, h * 64, wkey, cs=s0)
                slabs = [(c0, 384, [(0, 128), (128, 64), (192, 128), (320, 64)]) for c0 in range(0, 6144, 384)]
                self.gemm_fm(Lw["w_uq"], 8, slabs, cqn, "cqn", [(0, wd0)], epi_q)

                wv_d = Lw["w_ukv"].rearrange("(c p) n -> p c n", p=128)
                for c0 in range(0, 8192, 512):
                    h2 = c0 // 256
                    wb, wk = self.wbuf.next()
                    wv = wb[:, 0:4 * 512].rearrange("p (c n) -> p c n", n=512)
                    P.dma("pool", wv, wv_d[:, :, c0:c0 + 512], writes=[wk])
                    for hd in range(2):
                        pt, pk = self.psr.next()
                        for k in range(4):
                            P.op("pe", lambda e, pt=pt, wv=wv, k=k, hd=hd, ckvn=ckvn, wd0=wd0: e.matmul(
                                pt[:, 0:wd0], lhsT=wv[:, k, hd * 256:hd * 256 + 128], rhs=ckvn[:, k, 0:wd0], start=(k == 0), stop=(k == 3)),
                                reads=[wk, "ckvn"], writes=[pk], track=(k == 3))
                        self.qk_epilogue(l, False, False, t0, 128, s0, wd0, pt, pk, 0, self.kT, (h2 + hd) * 128, wkey, cs=s0)
                    for tt in range(wd0 // 128):
                        pt, pk = self.psr.next()
                        for hd in range(2):
                            for k in range(4):
                                P.op("pe", lambda e, pt=pt, wv=wv, k=k, hd=hd, tt=tt, ckvn=ckvn: e.matmul(
                                    pt[:, hd * 128:(hd + 1) * 128], lhsT=ckvn[:, k, tt * 128:(tt + 1) * 128],
                                    rhs=wv[:, k, hd * 256 + 128:hd * 256 + 256], start=(k == 0), stop=(k == 3)),
                                    reads=[wk, "ckvn"], writes=[pk], track=(k == 3 and hd == 1))
                        bt, bk = self.br.next()
                        P.op("act", lambda e, bt=bt, pt=pt: e.activation(out=bt[:, 0:256], in_=pt[:, 0:256], func=AF.Copy), reads=[pk], writes=[bk])
                        tok = t0 + s0 + tt * 128
                        P.dma("sp", self.vtm[tok:tok + 128, h2 * 128:(h2 + 2) * 128], bt[:, 0:256], reads=[bk, wkey])
        self.barrier("dve", [wkey])

    def stage3_oproj(self, l, src, src_key, dst, dst_key, need_ctx):
        P = self.P
        Lw = self.L[l]
        md = self.mods(l)
        slabs = [(c0, 256, [(0, 128), (128, 128)]) for c0 in range(0, D, 256)]
        for sbi, (t0, n, col) in enumerate(SBS):
            if col == 1 and not need_ctx:
                continue
            oT_sb = self.abuf[:, 0:DC * n].rearrange("p (c t) -> p c t", t=n)
            self.barrier("dve", ["abuf"])
            for c0 in range(0, DC, 8):
                P.dma("sp", oT_sb[:, c0:c0 + 8, :], self.oT[c0 * 128:(c0 + 8) * 128, t0:t0 + n].rearrange("(c p) t -> p c t", p=128),
                      reads=[("oW", l), "abuf"])
            self.barrier("dve", ["abuf"])
            self.resid_gemm(Lw["w_o"], DC, slabs, oT_sb, subs_of(n), 0, md["G1"], col, src, src_key, dst, dst_key, sbi, t0,
                            [("modv", l)])
            self.barrier("dve", [(dst_key, sbi)])

    def resid_gemm(self, w, KC, slabs, act, subs, soff, G, col, src, src_key, dst, dst_key, sbi, t0, modkeys):
        P = self.P

        def epi(c, M, s, wd, pt, pk):
            j = c // 128
            xt, xk = self.xr.next()
            tt = t0 + soff + s
            P.dma("sp", xt[:, :wd], src[c:c + 128, tt:tt + wd], reads=[(src_key, sbi)], writes=[xk])
            ft, fk = self.fr.next()
            P.op("dve", lambda e: e.scalar_tensor_tensor(out=ft[:, :wd], in0=pt[:, :wd], scalar=G[:, j, col:col + 1],
                                                         in1=xt[:, :wd], op0=ALU.mult, op1=ALU.add),
                 reads=[pk, xk] + modkeys, writes=[fk])
            P.dma("sp", dst[c:c + 128, tt:tt + wd], ft[:, :wd], reads=[fk, (dst_key, sbi)])
        self.gemm_fm(w, KC, slabs, act, "abuf", subs, epi)

    def stage4_ffn(self, l, src, src_key, dst, dst_key, need_ctx):
        P = self.P
        Lw = self.L[l]
        md = self.mods(l)
        P.dma("sp", self.cw[:], Lw["conv_w"], writes=["cw"])
        P.dma("sp", self.cb[:], Lw["conv_b"], writes=["cb"])
        wv_d = Lw["w_up"].rearrange("(c p) n -> p c n", p=128)
        abf, vbf = self.apad, self.vpad
        for sbi, (t0, n, col) in enumerate(SBS):
            if col == 1 and not need_ctx:
                continue
            lo = 1 if (col == 0 and t0 > 0) else 0
            hi = 1 if (col == 0 and t0 + n < NLAT) else 0
            ne = n + lo + hi
            hT = self.abuf[:, 0:DC * ne].rearrange("p (c t) -> p c t", t=ne)
            skeys = [(src_key, sbi)] + ([(src_key, sbi - 1)] if lo else []) + ([(src_key, sbi + 1)] if hi else [])
            self.norm_block(src, skeys, (t0 - lo, ne), md["A2"], md["B2"], col, hT, [("modv", l), ("A", l, 128)])
            ukey = ("uT", l, sbi)
            P.op("dve", lambda e: e.memset(abf[:, 0:1], 0.0), writes=["apad"])
            P.op("dve", lambda e, ne=ne: e.memset(abf[:, ne + 1:ne + 2], 0.0), writes=["apad"])
            subs_e = subs_of(ne)
            for m2 in range(0, FC, 2):
                wa, wak = self.wbuf.next()
                wav = wa[:, 0:DC * 256].rearrange("p (c n) -> p c n", n=256)
                P.dma("pool", wav, wv_d[:, :, m2 * 128:(m2 + 2) * 128], writes=[wak])
                wg, wgk = self.wbuf.next()
                wgv = wg[:, 0:DC * 256].rearrange("p (c n) -> p c n", n=256)
                P.dma("pool", wgv, wv_d[:, :, DFF + m2 * 128:DFF + (m2 + 2) * 128], writes=[wgk])
                for mi in range(2):
                    m = m2 + mi
                    for (s, wd) in subs_e:
                        pa, pak = self.psr.next()
                        for k in range(DC):
                            P.op("pe", lambda e, pa=pa, wav=wav, mi=mi, k=k, s=s, wd=wd, hT=hT: e.matmul(
                                pa[:, :wd], lhsT=wav[:, k, mi * 128:(mi + 1) * 128], rhs=hT[:, k, s:s + wd],
                                start=(k == 0), stop=(k == DC - 1)),
                                reads=[wak, "abuf"], writes=[pak], track=(k == DC - 1))
                        P.op("act", lambda e, pa=pa, s=s, wd=wd: e.activation(out=abf[:, 1 + s:1 + s + wd], in_=pa[:, :wd], func=AF.Copy),
                             reads=[pak], writes=["apad"])
                        pv, pvk = self.psr.next()
                        for k in range(DC):
                            P.op("pe", lambda e, pv=pv, wgv=wgv, mi=mi, k=k, s=s, wd=wd, hT=hT: e.matmul(
                                pv[:, :wd], lhsT=wgv[:, k, mi * 128:(mi + 1) * 128], rhs=hT[:, k, s:s + wd],
                                start=(k == 0), stop=(k == DC - 1)),
                                reads=[wgk, "abuf"], writes=[pvk], track=(k == DC - 1))
                        P.op("act", lambda e, pv=pv, s=s, wd=wd: e.activation(out=vbf[:, s:s + wd], in_=pv[:, :wd], func=AF.Copy),
                             reads=[pvk], writes=["vpad"])
                    if lo == 0:
                        P.op("dve", lambda e: e.memset(abf[:, 0:1], 0.0), writes=["apad"])
                    if hi == 0:
                        P.op("dve", lambda e, ne=ne: e.memset(abf[:, ne + 1:ne + 2], 0.0), writes=["apad"])
                    for (s, wd) in subs_of(n):
                        e0 = lo + s
                        ft, fk = self.fr.next()
                        P.op("dve", lambda e, ft=ft, e0=e0, wd=wd, m=m: e.tensor_scalar(
                            out=ft[:, :wd], in0=abf[:, e0:e0 + wd], scalar1=self.cw[:, m, 0:1], scalar2=self.cb[:, m:m + 1],
                            op0=ALU.mult, op1=ALU.add), reads=["apad", "cw", "cb"], writes=[fk])
                        P.op("dve", lambda e, ft=ft, e0=e0, wd=wd, m=m: e.scalar_tensor_tensor(
                            out=ft[:, :wd], in0=abf[:, e0 + 1:e0 + 1 + wd], scalar=self.cw[:, m, 1:2], in1=ft[:, :wd],
                            op0=ALU.mult, op1=ALU.add), reads=["apad", "cw", fk], writes=[fk])
                        P.op("dve", lambda e, ft=ft, e0=e0, wd=wd, m=m: e.scalar_tensor_tensor(
                            out=ft[:, :wd], in0=abf[:, e0 + 2:e0 + 2 + wd], scalar=self.cw[:, m, 2:3], in1=ft[:, :wd],
                            op0=ALU.mult, op1=ALU.add), reads=["apad", "cw", fk], writes=[fk])
                        f2, f2k = self.fr.next()
                        P.op("act", lambda e, ft=ft, f2=f2, wd=wd: e.activation(out=f2[:, :wd], in_=ft[:, :wd], func=AF.Silu),
                             reads=[fk], writes=[f2k])
                        bt, bk = self.br.next()
                        P.op("dve", lambda e, bt=bt, f2=f2, e0=e0, wd=wd: e.tensor_tensor(
                            out=bt[:, :wd], in0=f2[:, :wd], in1=vbf[:, e0:e0 + wd], op=ALU.mult),
                            reads=[f2k, "vpad"], writes=[bk])
                        P.dma("sp", self.uT[m * 128:(m + 1) * 128, t0 + s:t0 + s + wd], bt[:, :wd], reads=[bk, ukey])
            self.barrier("dve", [ukey])
            slabs = [(c0, 256, [(0, 128), (128, 128)]) for c0 in range(0, D, 256)]
            for (s, wd) in subs_of(n):
                u_sb = self.abuf[:, 0:FC * wd].rearrange("p (c t) -> p c t", t=wd)
                self.barrier("dve", ["abuf"])
                for c0 in range(0, FC, 8):
                    P.dma("sp", u_sb[:, c0:c0 + 8, :],
                          self.uT[c0 * 128:(c0 + 8) * 128, t0 + s:t0 + s + wd].rearrange("(c p) t -> p c t", p=128),
                          reads=[ukey, "abuf"])
                self.barrier("dve", ["abuf"])
                self.resid_gemm(Lw["w_down"], FC, slabs, u_sb, [(0, wd)], s, md["G2"], col, src, src_key, dst, dst_key, sbi, t0,
                                [("modv", l)])
            self.barrier("dve", [(dst_key, sbi)])

    def final_norm(self, src, src_key):
        P = self.P
        P.dma("sp", self.g_sb[:], self.fin_g, writes=["g_sb"])
        ps = self.ps[6]
        for sbi, (t0, n, col) in enumerate(SBS):
            if col == 1:
                continue
            for (s, w) in subs_of(n):
                for j in range(DC):
                    xt, xk = self.xr.next()
                    P.dma("sp", xt[:, :w], src[j * 128:(j + 1) * 128, t0 + s:t0 + s + w], reads=[(src_key, sbi)], writes=[xk])
                    ft, fk = self.fr.next()
                    P.op("act", lambda e, ft=ft, xt=xt, w=w: e.activation(out=ft[:, :w], in_=xt[:, :w], func=AF.Square), reads=[xk], writes=[fk])
                    P.op("pe", lambda e, ft=ft, w=w, j=j: e.matmul(ps[:, :w], lhsT=self.ones_f[:], rhs=ft[:, :w], start=(j == 0), stop=(j == DC - 1)),
                         reads=[fk, "ones_f"], writes=[self.pk[6]])
                ft, fk = self.fr.next()
                P.op("act", lambda e, ft=ft, w=w: e.activation(out=ft[:, :w], in_=ps[:, :w], func=AF.Sqrt, scale=1.0 / D, bias=self.eps_ap),
                     reads=[self.pk[6], "eps"], writes=[fk])
                P.op("dve", lambda e, ft=ft, w=w: e.reciprocal(out=self.rstd[:, :w], in_=ft[:, :w]), reads=[fk], writes=["rstd"])
                for j in range(DC):
                    xt, xk = self.xr.next()
                    P.dma("sp", xt[:, :w], src[j * 128:(j + 1) * 128, t0 + s:t0 + s + w], reads=[(src_key, sbi)], writes=[xk])
                    ft, fk = self.fr.next()
                    P.op("dve", lambda e, ft=ft, xt=xt, w=w, j=j: e.scalar_tensor_tensor(
                        out=ft[:, :w], in0=xt[:, :w], scalar=self.g_sb[:, j:j + 1], in1=self.rstd[:, :w], op0=ALU.mult, op1=ALU.mult),
                        reads=[xk, "rstd", "g_sb"], writes=[fk])
                    P.dma("sp", self.out[j * 128:(j + 1) * 128, t0 + s:t0 + s + w], ft[:, :w], reads=[fk])

    def dump(self, src, src_key, ncols=NLAT, c0=0):
        P = self.P
        allk = [(src_key, i) for i in range(len(SBS))]
        for (s, w) in subs_of(ncols):
            for j in range(DC):
                xt, xk = self.xr.next()
                P.dma("sp", xt[:, :w], src[j * 128:(j + 1) * 128, c0 + s:c0 + s + w], reads=allk, writes=[xk])
                P.dma("sp", self.out[j * 128:(j + 1) * 128, s:s + w], xt[:, :w], reads=[xk])

    def build(self):
        P = self.P
        self.esink = None
        self.setup_consts()
        self.phase_mod(0)
        cur, cur_key = self.xT, "xin"
        if self.stop_after == "mod":
            P.dma("sp", self.out[0:128, 0:384], self.modv[0][:].rearrange("p m j -> p (m j)"), reads=[("modv", 0)])
            P.dma("sp", self.out[128:256, 0:64], self.A1[0][:].rearrange("p m j -> p (m j)"), reads=[("A", 0, 32)])
            P.finish()
            return P
        for l in range(self.n_layers):
            kind = l % 4
            need_ctx = l < 3
            self.barrier("dve", [])
            self.barrier("act", [])
            if l > 0:
                P.switch_sems()
            if l + 1 < self.n_layers:
                per = 2 if kind in (0, 1) else 1
                self.bg = (lambda l=l, per=per: self.mod_slabs(l + 1, per))
            else:
                self.bg = None
            if kind == 0:
                self.stage1_gqa(l, kind, cur, cur_key)
                self.attention(l, 8, 4, lambda q0, is_ctx: ([32, 33] if is_ctx else [32, 33] + list(range(32))),
                               128 ** -0.5, need_ctx)
            elif kind == 1:
                self.stage1_gqa(l, kind, cur, cur_key)
                self.attention_window(l, need_ctx)
            elif kind == 2:
                self.stage1_gqa(l, kind, cur, cur_key)
                self.attention_nbr(l, need_ctx)
            else:
                self.stage1_mla(l, cur, cur_key)
                self.attention(l, 32, 1, lambda q0, is_ctx: ([32, 33] if is_ctx else [32, 33] + list(range(32))),
                               192 ** -0.5, need_ctx, mla=True)
            self.bg = None
            if l + 1 < self.n_layers:
                self.phase_mod(l + 1)
            if self.stop_after in ("s1", "attn"):
                if self.stop_after == "s1":
                    srcs = [(self.qT, 0, 1024, ("qkvW", l)), (self.kT, 1024, 1024, ("qkvW", l))]
                else:
                    srcs = [(self.oT, 0, 2048, ("oW", l))]
                for (sd, o0, nr, key) in srcs:
                    for r0 in range(0, nr, 128):
                        for (s_, w_) in subs_of(NLAT):
                            bt, bk = self.br.next()
                            P.dma("sp", bt[:, :w_], sd[r0:r0 + 128, s_:s_ + w_], reads=[key], writes=[bk])
                            ft, fk = self.fr.next()
                            P.op("act", lambda e, ft=ft, bt=bt, w_=w_: e.activation(out=ft[:, :w_], in_=bt[:, :w_], func=AF.Copy), reads=[bk], writes=[fk])
                            P.dma("sp", self.out[o0 + r0:o0 + r0 + 128, s_:s_ + w_], ft[:, :w_], reads=[fk])
                P.finish()
                return P
            self.stage3_oproj(l, cur, cur_key, self.xm, "xm", need_ctx)
            self.stage4_ffn(l, self.xm, "xm", self.xs, "xs", need_ctx)
            cur, cur_key = self.xs, "xs"
        if self.debug_out == "x":
            self.dump(cur, cur_key)
        elif self.debug_out == "cx":
            self.dump(cur, cur_key, NCTX, NLAT)
        else:
            self.final_norm(cur, cur_key)
        P.finish()
        return P


def _chunk_layout(v):
    v = np.asarray(v, np.float32)
    return np.ascontiguousarray(v.reshape(-1, 128).T)


def _rope_tables(rot_dim):
    n_freq = rot_dim // 4
    freqs = (10000.0 ** (-np.arange(n_freq, dtype=np.float64) / n_freq))
    t = np.arange(NLAT)
    row = (t // 64).astype(np.float64)
    colp = (t % 64).astype(np.float64)
    ang = np.concatenate([row[:, None] * freqs, colp[:, None] * freqs], axis=-1)
    ang = ang.astype(np.float32).astype(np.float64)
    cos = np.cos(ang).T
    sin = np.sin(ang).T
    C = np.concatenate([cos, cos], 0)
    S = np.concatenate([sin, sin], 0)
    C = np.concatenate([C, np.ones((rot_dim, NCTX))], 1)
    S = np.concatenate([S, np.zeros((rot_dim, NCTX))], 1)
    return np.ascontiguousarray(C, np.float32), np.ascontiguousarray(S, np.float32)


def _rot_matrix(n):
    h = n // 2
    R = np.zeros((n, n), np.float32)
    for m in range(h):
        R[m + h, m] = -1.0
        R[m, m + h] = 1.0
    return R


INPUT_NAMES = (
    "x", "c", "ctx", "c_ctx", "final_norm_g",
    "l0_ada_w", "l0_ada_b", "l0_attn_norm_g", "l0_w_qkv", "l0_q_norm_g", "l0_k_norm_g", "l0_w_o",
    "l0_ffn_norm_g", "l0_ffn_w_up", "l0_ffn_conv_w", "l0_ffn_conv_b", "l0_ffn_w_down",
    "l1_ada_w", "l1_ada_b", "l1_attn_norm_g", "l1_w_qkv", "l1_sink", "l1_w_o",
    "l1_ffn_norm_g", "l1_ffn_w_up", "l1_ffn_conv_w", "l1_ffn_conv_b", "l1_ffn_w_down",
    "l2_ada_w", "l2_ada_b", "l2_attn_norm_g", "l2_w_qkv", "l2_rel_bias", "l2_w_o",
    "l2_ffn_norm_g", "l2_ffn_w_up", "l2_ffn_conv_w", "l2_ffn_conv_b", "l2_ffn_w_down",
    "l3_ada_w", "l3_ada_b", "l3_attn_norm_g", "l3_w_dq", "l3_q_norm_g", "l3_w_uq", "l3_w_dkv",
    "l3_kv_norm_g", "l3_w_ukv", "l3_w_o",
    "l3_ffn_norm_g", "l3_ffn_w_up", "l3_ffn_conv_w", "l3_ffn_conv_b", "l3_ffn_w_down",
)


def host_inputs(inputs):
    missing = [n for n in INPUT_NAMES if n not in inputs]
    assert not missing, missing
    shared = {}
    C, S = _rope_tables(128)
    shared["ropeC"], shared["ropeS"] = C, S
    C64, S64 = _rope_tables(64)
    shared["ropeC64"], shared["ropeS64"] = C64, S64
    shared["rotm"] = _rot_matrix(128)
    shared["rotm64"] = _rot_matrix(64)
    kk = np.arange(128)[:, None]
    qq = np.arange(512)[None, :]
    shared["wmask"] = np.stack([(np.abs(qq - kk - (di - 1) * 128) <= 128) for di in range(6)], 0).astype(np.float32)
    krl, kc = np.arange(128)[:, None] // 64, np.arange(128)[:, None] % 64
    qrl, qc = np.arange(512)[None, :] // 64, np.arange(512)[None, :] % 64
    nm = np.zeros((3, 8, 128, 512), np.float32)
    NB_DR = np.zeros((8, 128, 512), np.int64)
    NB_DC = np.zeros((8, 128, 512), np.int64)
    for ty, j in enumerate((0, 3, 7)):
        for kr_ in range(8):
            krow = 8 * j - 4 + 2 * kr_ + krl
            qrow = 8 * j + qrl
            rs = np.clip(qrow - 4, 0, 56)
            cs_ = np.clip(qc - 8, 0, 48)
            inw = (krow >= rs) & (krow < rs + 8) & (kc >= cs_) & (kc < cs_ + 16) & (krow >= 0) & (krow < 64)
            nm[ty, kr_] = inw
            if ty == 1:
                NB_DR[kr_] = np.clip(krow - qrow + 7, 0, 14)
                NB_DC[kr_] = np.clip(kc - qc + 15, 0, 30)
    shared["nmask"] = nm
    shared["fin_g"] = _chunk_layout(inputs["final_norm_g"])
    for l in range(4):
        p = "l%d_" % l
        shared[p + "ada_w"] = np.asarray(inputs[p + "ada_w"], np.float32)
        shared[p + "ada_b"] = _chunk_layout(inputs[p + "ada_b"])
        shared[p + "attn_g"] = _chunk_layout(inputs[p + "attn_norm_g"])
        shared[p + "ffn_g"] = _chunk_layout(inputs[p + "ffn_norm_g"])
        if l < 3:
            shared[p + "w_qkv"] = np.asarray(inputs[p + "w_qkv"], np.float32)
        if l == 0:
            shared[p + "qk_g"] = np.ascontiguousarray(np.stack([inputs[p + "q_norm_g"], inputs[p + "k_norm_g"]], 1), np.float32)
        if l == 1:
            shared[p + "sink"] = np.ascontiguousarray(np.broadcast_to(np.asarray(inputs[p + "sink"], np.float32)[None, :], (128, 32)))
        if l == 2:
            rb = np.asarray(inputs[p + "rel_bias"], np.float32)
            shared[p + "relb"] = np.ascontiguousarray(rb[:, NB_DR, NB_DC])
        if l == 3:
            shared[p + "w_dq"] = np.asarray(inputs[p + "w_dq"], np.float32)
            shared[p + "q_g"] = _chunk_layout(inputs[p + "q_norm_g"])
            shared[p + "w_uq"] = np.asarray(inputs[p + "w_uq"], np.float32)
            shared[p + "w_dkv"] = np.asarray(inputs[p + "w_dkv"], np.float32)
            shared[p + "kv_g"] = _chunk_layout(inputs[p + "kv_norm_g"])
            shared[p + "w_ukv"] = np.asarray(inputs[p + "w_ukv"], np.float32)
        shared[p + "w_o"] = np.asarray(inputs[p + "w_o"], np.float32)
        shared[p + "w_up"] = np.asarray(inputs[p + "ffn_w_up"], np.float32)
        cw = np.asarray(inputs[p + "ffn_conv_w"], np.float32)
        shared[p + "conv_w"] = np.ascontiguousarray(cw.reshape(3, FC, 128).transpose(2, 1, 0))
        shared[p + "conv_b"] = _chunk_layout(inputs[p + "ffn_conv_b"])
        shared[p + "w_down"] = np.asarray(inputs[p + "ffn_w_down"], np.float32)
    maps = []
    x = np.asarray(inputs["x"], np.float32)
    ctx = np.asarray(inputs["ctx"], np.float32)
    c = np.asarray(inputs["c"], np.float32)
    cc = np.asarray(inputs["c_ctx"], np.float32)
    for b in range(NCORES):
        m = dict(shared)
        m["xT"] = np.ascontiguousarray(np.concatenate([x[b], ctx[b]], 0).T)
        m["cvec"] = np.ascontiguousarray(np.stack([_chunk_layout(c[b]), _chunk_layout(cc)], -1))
        maps.append(m)
    return maps


def kernel(**inputs):
    B = Builder(n_layers=4)
    P = B.build()
    maps = host_inputs(inputs)
    maps = [{k: v for k, v in m.items() if k in B.inp} for m in maps]
    res = run_bass_kernel_spmd(P.nc, maps, core_ids=list(range(NCORES)))
    out = np.stack([np.ascontiguousarray(res.results[b]["outT"].T) for b in range(NCORES)], 0)
    P.es.close()
    return out.astype(np.float32)
```

```python
import numpy as np
from contextlib import ExitStack
import concourse.bass as bass
import concourse.mybir as mybir
from concourse.bass_utils import run_bass_kernel_spmd

F32 = mybir.dt.float32
BF16 = mybir.dt.bfloat16
AF = mybir.ActivationFunctionType
ALU = mybir.AluOpType

D = 4096
DC = 32
NLAT = 4096
NCTX = 256
T = NLAT + NCTX
DFF = 6144
FC = 48
EPS = 1e-6
NCORES = 4
N_LANES = {"sp": 24, "pool": 12}
SBS = [(0, 1024, 0), (1024, 1024, 0), (2048, 1024, 0), (3072, 1024, 0), (4096, 256, 1)]


def subs_of(n, step=None):
    if step is None:
        k = -(-n // 512)
        step = -(-n // k)
    return [(s, min(step, n - s)) for s in range(0, n, step)]


class Prog:
    def __init__(self):
        self.nc = bass.Bass("TRN2", target_bir_lowering=False)
        self.es = ExitStack()
        nc = self.nc
        self.eng_names = ["pe", "act", "dve", "pool", "sp"]
        self.sem, self.cnt, self.seen, self.insts = {}, {}, {}, {}
        for n in self.eng_names:
            self.sem[n] = self.es.enter_context(nc.semaphore("c_" + n))
            self.cnt[n] = 0
            self.seen[n] = {}
            self.insts[n] = []
        self.lanes, self.lane_rr = {}, {}
        for q, nl in N_LANES.items():
            self.lanes[q] = [[self.es.enter_context(nc.semaphore("d_%s%d" % (q, i))), 0] for i in range(nl)]
            self.lane_rr[q] = 0
        self.buf = {}
        self.n_inst = 0
        self.done_sems = []
        self.nsw = 0

    def sbuf(self, name, shape, dt):
        return self.es.enter_context(self.nc.sbuf_tensor(name, list(shape), dt))

    def switch_sems(self, engs=("pe", "act", "dve")):
        for n in engs:
            self.done_sems.append((self.sem[n], self.cnt[n]))
            self.nsw += 1
            self.sem[n] = self.es.enter_context(self.nc.semaphore("c_%s_%d" % (n, self.nsw)))
            self.cnt[n] = 0

    def psum(self, name, shape, dt):
        return self.es.enter_context(self.nc.psum_tensor(name, list(shape), dt))

    def _deps(self, reads, writes):
        deps = []
        for k in reads:
            b = self.buf.get(k)
            if b and b[0] is not None:
                deps.append(b[0])
        for k in writes:
            b = self.buf.get(k)
            if b:
                if b[0] is not None:
                    deps.append(b[0])
                deps.extend(b[1])
        return deps

    def _commit(self, ev, reads, writes):
        for k in reads:
            b = self.buf.setdefault(k, [None, []])
            b[1].append(ev)
            if len(b[1]) > 48:
                mx = {}
                for s, v in b[1]:
                    if mx.get(id(s), (None, -1))[1] < v:
                        mx[id(s)] = (s, v)
                b[1] = list(mx.values())
        for k in writes:
            self.buf[k] = [ev, []]

    def _waits(self, eng, deps, same_ok):
        seen = self.seen[eng]
        ws = {}
        for s, v in deps:
            if same_ok and s is self.sem[eng]:
                continue
            if seen.get(id(s), 0) >= v:
                continue
            if ws.get(id(s), (None, 0))[1] < v:
                ws[id(s)] = (s, v)
        for s, v in ws.values():
            seen[id(s)] = v
        return list(ws.values())

    def op(self, eng, fn, reads=(), writes=(), track=True):
        deps = self._deps(reads, writes)
        waits = self._waits(eng, deps, same_ok=(eng == "pe"))
        if track:
            self.cnt[eng] += 1
            ev = (self.sem[eng], self.cnt[eng])
            inc = (self.sem[eng], 1)
        else:
            ev = (self.sem[eng], self.cnt[eng] + 1)
            inc = None
        self.insts[eng].append((waits, fn, inc))
        self._commit(ev, reads, writes)
        self.n_inst += 1
        return ev

    def dma(self, q, out, in_, reads=(), writes=(), **kw):
        lanes = self.lanes[q]
        li = self.lane_rr[q]
        self.lane_rr[q] = (li + 1) % len(lanes)
        lane = lanes[li]
        deps = self._deps(reads, writes)
        if lane[1] > 0:
            deps.append((lane[0], lane[1]))
        waits = self._waits(q, deps, same_ok=False)
        lane[1] += 16
        ev = (lane[0], lane[1])

        def fn(e, out=out, in_=in_, kw=kw):
            return e.dma_start(out=out, in_=in_, **kw)
        self.insts[q].append((waits, fn, (lane[0], 16)))
        self._commit(ev, reads, writes)
        self.n_inst += 1
        return ev

    def finish(self):
        nc = self.nc
        final = []
        for q in self.lanes:
            for s, v in self.lanes[q]:
                if v > 0:
                    final.append((s, v))
        for n in self.eng_names:
            if self.cnt[n] > 0:
                final.append((self.sem[n], self.cnt[n]))
        final.extend([(s, v) for (s, v) in self.done_sems if v > 0])
        insts = self.insts

        def run(e, lst, extra=()):
            for waits, fn, inc in lst:
                for s, v in waits:
                    e.wait_ge(s, v)
                ins = fn(e)
                if inc is not None:
                    ins.then_inc(inc[0], inc[1])
            for s, v in extra:
                e.wait_ge(s, v)

        with nc.Block() as block:
            @block.tensor
            def _(e):
                run(e, insts["pe"])

            @block.scalar
            def _(e):
                run(e, insts["act"])

            @block.vector
            def _(e):
                run(e, insts["dve"])

            @block.gpsimd
            def _(e):
                run(e, insts["pool"])

            @block.sync
            def _(e):
                run(e, insts["sp"], extra=final)
        return nc


class Ring:
    def __init__(self, name, tiles, keys=None):
        self.name, self.tiles, self.i = name, tiles, 0
        self.keys = keys if keys is not None else [(name, k) for k in range(len(tiles))]

    def next(self):
        k = self.i % len(self.tiles)
        self.i += 1
        return self.tiles[k], self.keys[k]


class Builder:
    def __init__(self, n_layers=4, debug_out=None, stop_after=None):
        self.stop_after = stop_after
        self.P = Prog()
        self.n_layers = n_layers
        self.debug_out = debug_out
        self.inp = {}
        self.alloc()

    def din(self, name, shape, dt=F32):
        t = self.P.nc.dram_tensor(name, list(shape), dt, kind="ExternalInput").ap()
        self.inp[name] = t
        return t

    def dscr(self, name, shape, dt):
        return self.P.nc.dram_tensor(name, list(shape), dt, kind="Internal").ap()

    def alloc(self):
        P = self.P
        self.xT = self.din("xT", [D, T])
        self.cvec = self.din("cvec", [128, DC, 2])
        self.ropeC = self.din("ropeC", [128, T])
        self.ropeS = self.din("ropeS", [128, T])
        self.ropeC64 = self.din("ropeC64", [64, T])
        self.ropeS64 = self.din("ropeS64", [64, T])
        self.rotm = self.din("rotm", [128, 128])
        self.rotm64 = self.din("rotm64", [64, 64])
        self.wmask = self.din("wmask", [6, 128, 512])
        self.nmask = self.din("nmask", [3, 8, 128, 512])
        self.fin_g = self.din("fin_g", [128, DC])
        self.L = []
        for l in range(self.n_layers):
            p = "l%d_" % l
            d = {}
            d["ada_w"] = self.din(p + "ada_w", [D, 6 * D])
            d["ada_b"] = self.din(p + "ada_b", [128, 192])
            d["attn_g"] = self.din(p + "attn_g", [128, DC])
            d["ffn_g"] = self.din(p + "ffn_g", [128, DC])
            if l < 3:
                d["w_qkv"] = self.din(p + "w_qkv", [D, 6144])
            if l == 0:
                d["qk_g"] = self.din(p + "qk_g", [128, 2])
            if l == 1:
                d["sink"] = self.din(p + "sink", [128, 32])
            if l == 2:
                d["relb"] = self.din(p + "relb", [32, 8, 128, 512])
            if l == 3:
                d["w_dq"] = self.din(p + "w_dq", [D, 1024])
                d["q_g"] = self.din(p + "q_g", [128, 8])
                d["w_uq"] = self.din(p + "w_uq", [1024, 6144])
                d["w_dkv"] = self.din(p + "w_dkv", [D, 576])
                d["kv_g"] = self.din(p + "kv_g", [128, 4])
                d["w_ukv"] = self.din(p + "w_ukv", [512, 8192])
            d["w_o"] = self.din(p + "w_o", [D, D])
            d["w_up"] = self.din(p + "w_up", [D, 2 * DFF])
            d["conv_w"] = self.din(p + "conv_w", [128, FC, 3])
            d["conv_b"] = self.din(p + "conv_b", [128, FC])
            d["w_down"] = self.din(p + "w_down", [DFF, D])
            self.L.append(d)
        self.out = P.nc.dram_tensor("outT", [D, NLAT], F32, kind="ExternalOutput").ap()
        self.xs = self.dscr("xs", [D, T], F32)
        self.xm = self.dscr("xm", [D, T], F32)
        self.qT = self.dscr("qT", [D, T], BF16)
        self.qpT = self.dscr("qpT", [2048, T], BF16)
        self.kT = self.dscr("kT", [D + 64, T], BF16)
        self.vtm = self.dscr("vtm", [T, D], BF16)
        self.oT = self.dscr("oT", [D, T], BF16)
        self.uT = self.dscr("uT", [DFF, T], BF16)
        self.abuf = P.sbuf("abuf", [128, 33792], BF16)
        self.wbuf = Ring("w", [P.sbuf("wbuf%d" % i, [128, 8192], BF16) for i in range(4)])
        self.xr = Ring("x", [P.sbuf("xt%d" % i, [128, 512], F32) for i in range(4)])
        self.fr = Ring("f", [P.sbuf("ft%d" % i, [128, 512], F32) for i in range(8)])
        self.br = Ring("b", [P.sbuf("bt%d" % i, [128, 512], BF16) for i in range(4)])
        self.rstd = P.sbuf("rstd", [128, 512], F32)
        self.ones_f = P.sbuf("ones_f", [128, 128], F32)
        self.ones_b = P.sbuf("ones_b", [128, 128], BF16)
        self.rot_sb = P.sbuf("rot_sb", [128, 128], F32)
        self.rot64_sb = P.sbuf("rot64_sb", [64, 64], F32)
        self.cos_sb = P.sbuf("cos_sb", [128, 1024], F32)
        self.sin_sb = P.sbuf("sin_sb", [128, 1024], F32)
        self.scT = P.sbuf("scT", [128, DC, 2], BF16)
        self.cv_sb = P.sbuf("cv_sb", [128, DC, 2], F32)
        self.modv = [P.sbuf("modv%d" % l, [128, 192, 2], F32) for l in range(4)]
        self.A1 = [P.sbuf("A1_%d" % l, [128, DC, 2], F32) for l in range(4)]
        self.A2 = [P.sbuf("A2_%d" % l, [128, DC, 2], F32) for l in range(4)]
        self.small = P.sbuf("small", [128, 256], F32)
        self.cw = P.sbuf("cw", [128, FC, 3], F32)
        self.cb = P.sbuf("cb", [128, FC], F32)
        self.g_sb = P.sbuf("g_sb", [128, DC], F32)
        self.eps_sb = P.sbuf("eps_sb", [128, 1], F32)
        self.dummy = P.sbuf("bar_dmy", [128, 8], F32)
        self.apad = P.sbuf("apad", [128, 1028], F32)
        self.vpad = P.sbuf("vpad", [128, 1028], F32)
        self.eps_ap = self.eps_sb[:, 0:1]
        if self.n_layers > 3:
            self.mbuf = P.sbuf("mbuf", [128, 8, 256], F32)
            self.cqn = P.sbuf("cqn", [128, 8, 256], BF16)
            self.ckvn = P.sbuf("ckvn", [128, 4, 256], BF16)
        self.ps = [P.psum("ps%d" % i, [128, 512], F32) for i in range(8)]
        self.pk = [("psb", i) for i in range(8)]
        self.psr = Ring("ps", self.ps[0:4], self.pk[0:4])
        self.nbar = 0
        self.mod_state = {}
        self.mod_done = set()
        self.bg = None
        self.bg_sched = []

    def barrier(self, eng, keys, reads=()):
        i = self.nbar % 8
        self.nbar += 1
        if eng == "act":
            fn = lambda e, i=i: e.activation(out=self.dummy[:, i:i + 1], in_=self.eps_sb[:, 0:1], func=AF.Copy)
        else:
            fn = lambda e, i=i: e.memset(self.dummy[:, i:i + 1], 0.0)
        return self.P.op(eng, fn, reads=list(reads), writes=list(keys))

    def setup_consts(self):
        P = self.P
        P.op("pool", lambda e: e.memset(self.ones_f[:], 1.0), writes=["ones_f"])
        P.op("pool", lambda e: e.memset(self.ones_b[:], 1.0), writes=["ones_b"])
        P.op("pool", lambda e: e.memset(self.eps_sb[:], EPS), writes=["eps"])
        P.dma("sp", self.rot_sb[:], self.rotm, writes=["rot"])
        P.dma("sp", self.rot64_sb[:], self.rotm64, writes=["rot64"])
        P.dma("sp", self.cv_sb[:], self.cvec, writes=["cv"])
        P.op("act", lambda e: e.activation(out=self.scT[:], in_=self.cv_sb[:], func=AF.Silu),
             reads=["cv"], writes=["scT"])

    def mod_slabs(self, l, count):
        P = self.P
        Lw = self.L[l]
        adaw = Lw["ada_w"].rearrange("(c p) n -> p c n", p=128)
        ps = self.ps[7]
        st = self.mod_state.setdefault(l, {"next": 0})
        for _ in range(count):
            slab = st["next"]
            if slab >= 96:
                return
            st["next"] += 1
            wb, wk = self.wbuf.next()
            wv = wb[:, 0:DC * 256].rearrange("p (c n) -> p c n", n=256)
            P.dma("pool", wv, adaw[:, :, slab * 256:(slab + 1) * 256], writes=[wk])
            for mi in range(2):
                m = slab * 2 + mi
                for k in range(DC):
                    P.op("pe", lambda e, wv=wv, mi=mi, k=k, m=m: e.matmul(
                        ps[:, 2 * m:2 * m + 2], lhsT=wv[:, k, mi * 128:(mi + 1) * 128], rhs=self.scT[:, k, :],
                        start=(k == 0), stop=(k == DC - 1)),
                        reads=[wk, "scT"], writes=[self.pk[7]], track=(k == DC - 1))

    def phase_mod(self, l):
        if l in self.mod_done:
            return
        self.mod_done.add(l)
        P = self.P
        Lw = self.L[l]
        ps = self.ps[7]
        self.mod_slabs(l, 96)
        mv = self.modv[l]
        ab = self.small
        P.dma("sp", ab[:, 0:192], Lw["ada_b"], writes=["small"])
        psv = ps[:, 0:384].rearrange("p (m j) -> p m j", j=2)
        for col in range(2):
            P.op("dve", lambda e, col=col: e.tensor_tensor(out=mv[:, :, col], in0=psv[:, :, col], in1=ab[:, 0:192], op=ALU.add),
                 reads=[self.pk[7], "small"], writes=[("modv", l)])
        for (gname, dst, base) in (("attn_g", self.A1[l], 32), ("ffn_g", self.A2[l], 128)):
            P.dma("sp", self.g_sb[:], Lw[gname], writes=["g_sb"])
            for col in range(2):
                P.op("dve", lambda e, col=col, dst=dst, base=base: e.scalar_tensor_tensor(
                    out=dst[:, :, col], in0=mv[:, base:base + DC, col], scalar=1.0, in1=self.g_sb[:],
                    op0=ALU.add, op1=ALU.mult), reads=[("modv", l), "g_sb"], writes=[("A", l, base)])

    def bg_step(self):
        while self.bg_sched:
            l2, per = self.bg_sched[0]
            st = self.mod_state.setdefault(l2, {"next": 0})
            if st["next"] < 96:
                self.mod_slabs(l2, per)
                if st["next"] >= 96:
                    self.phase_mod(l2)
                    self.bg_sched.pop(0)
                return
            self.phase_mod(l2)
            self.bg_sched.pop(0)

    def mods(self, l):
        mv = self.modv[l]
        return dict(A1=self.A1[l], B1=mv[:, 0:32, :], G1=mv[:, 64:96, :],
                    A2=self.A2[l], B2=mv[:, 96:128, :], G2=mv[:, 160:192, :])

    def norm_block(self, src, src_keys, cols, A, B, col, dst, modkeys):
        P = self.P
        t0, n = cols
        ps = self.ps[6]
        self.barrier("act", ["abuf"])
        for (s, w) in subs_of(n):
            for j in range(DC):
                xt, xk = self.xr.next()
                P.dma("sp", xt[:, :w], src[j * 128:(j + 1) * 128, t0 + s:t0 + s + w], reads=src_keys, writes=[xk])
                ft, fk = self.fr.next()
                P.op("act", lambda e, ft=ft, xt=xt, w=w: e.activation(out=ft[:, :w], in_=xt[:, :w], func=AF.Square),
                     reads=[xk], writes=[fk])
                P.op("pe", lambda e, ft=ft, w=w, j=j: e.matmul(ps[:, :w], lhsT=self.ones_f[:], rhs=ft[:, :w],
                                                             start=(j == 0), stop=(j == DC - 1)),
                     reads=[fk, "ones_f"], writes=[self.pk[6]])
            ft, fk = self.fr.next()
            P.op("act", lambda e, ft=ft, w=w: e.activation(out=ft[:, :w], in_=ps[:, :w], func=AF.Sqrt,
                                                         scale=1.0 / D, bias=self.eps_ap),
                 reads=[self.pk[6], "eps"], writes=[fk])
            P.op("dve", lambda e, ft=ft, w=w: e.reciprocal(out=self.rstd[:, :w], in_=ft[:, :w]),
                 reads=[fk], writes=["rstd"])
            for j in range(DC):
                xt, xk = self.xr.next()
                P.dma("sp", xt[:, :w], src[j * 128:(j + 1) * 128, t0 + s:t0 + s + w], reads=src_keys, writes=[xk])
                ft, fk = self.fr.next()
                P.op("dve", lambda e, ft=ft, xt=xt, w=w: e.tensor_tensor(out=ft[:, :w], in0=xt[:, :w], in1=self.rstd[:, :w], op=ALU.mult),
                     reads=[xk, "rstd"], writes=[fk])
                P.op("act", lambda e, ft=ft, w=w, j=j, s=s: e.activation(
                    out=dst[:, j, s:s + w], in_=ft[:, :w], func=AF.Identity,
                    scale=A[:, j, col:col + 1], bias=B[:, j, col:col + 1]),
                    reads=[fk, "abuf"] + modkeys)
        self.barrier("act", ["abuf"])

    def gemm_fm(self, w, KC, slabs, act, act_key, subs, epi):
        P = self.P
        wv_d = w.rearrange("(c p) n -> p c n", p=128)
        kparts = [(0, KC)] if KC <= 32 else [(0, KC // 2), (KC // 2, KC - KC // 2)]
        pend = []
        for (c0, ncols, chunks) in slabs:
            wvs, wks = [], []
            for (k0, kn) in kparts:
                wb, wk = self.wbuf.next()
                wv = wb[:, 0:kn * ncols].rearrange("p (c n) -> p c n", n=ncols)
                P.dma("pool", wv, wv_d[:, k0:k0 + kn, c0:c0 + ncols], writes=[wk])
                wvs.append(wv)
                wks.append(wk)
            for (off, M) in chunks:
                for (s, wd) in subs:
                    pt, pk = self.psr.next()
                    for k in range(KC):
                        pi = 0 if k < kparts[0][1] else 1
                        kk = k - kparts[pi][0]
                        P.op("pe", lambda e, pt=pt, wv=wvs[pi], kk=kk, off=off, M=M, k=k, s=s, wd=wd: e.matmul(
                            pt[0:M, :wd], lhsT=wv[:, kk, off:off + M], rhs=act[:, k, s:s + wd],
                            start=(k == 0), stop=(k == KC - 1)),
                            reads=[wks[pi], act_key], writes=[pk], track=(k == KC - 1))
                    for gen in list(pend):
                        try:
                            next(gen)
                        except StopIteration:
                            pend.remove(gen)
                    r = epi(c0 + off, M, s, wd, pt, pk)
                    if r is not None:
                        try:
                            next(r)
                            pend.append(r)
                        except StopIteration:
                            pass
        while pend:
            for gen in list(pend):
                try:
                    next(gen)
                except StopIteration:
                    pend.remove(gen)

    def gemm_tm(self, w, KC, c0, ncols, act, act_key, ntiles, epi):
        P = self.P
        wv_d = w.rearrange("(c p) n -> p c n", p=128)
        for g0 in range(0, ncols, 256):
            wb, wk = self.wbuf.next()
            wv = wb[:, 0:KC * 256].rearrange("p (c n) -> p c n", n=256)
            P.dma("pool", wv, wv_d[:, :, c0 + g0:c0 + g0 + 256], writes=[wk])
            for tt in range(ntiles):
                pt, pk = self.psr.next()
                for k in range(KC):
                    P.op("pe", lambda e, pt=pt, wv=wv, k=k, tt=tt: e.matmul(
                        pt[:, 0:256], lhsT=act[:, k, tt * 128:(tt + 1) * 128], rhs=wv[:, k, :],
                        start=(k == 0), stop=(k == KC - 1)),
                        reads=[wk, act_key], writes=[pk], track=(k == KC - 1))
                epi(g0, tt, pt, pk)

    def qk_epilogue(self, *a, **kw):
        for _ in self.qk_epi_gen(*a, **kw):
            pass

    def qk_epi_gen(self, l, do_norm, do_rope, t0, M, s, wd, pt, pk, gcol, dst, drow, wkey, cs=None):
        cs = s if cs is None else cs
        P = self.P
        if not do_norm and not do_rope:
            bt, bk = self.br.next()
            P.op("act", lambda e: e.activation(out=bt[0:M, :wd], in_=pt[0:M, :wd], func=AF.Copy), reads=[pk], writes=[bk])
            P.dma("sp", dst[drow:drow + M, t0 + s:t0 + s + wd], bt[0:M, :wd], reads=[bk, wkey])
            return
        qf, qfk = self.fr.next()
        if do_norm:
            sq, sqk = self.fr.next()
            P.op("act", lambda e: e.activation(out=sq[:, :wd], in_=pt[:, :wd], func=AF.Square), reads=[pk], writes=[sqk])
            p2 = self.ps[4]
            yield
            P.op("pe", lambda e: e.matmul(p2[:, :wd], lhsT=self.ones_f[:], rhs=sq[:, :wd], start=True, stop=True),
                 reads=[sqk, "ones_f"], writes=[self.pk[4]])
            P.op("act", lambda e: e.activation(out=sq[:, :wd], in_=p2[:, :wd], func=AF.Sqrt, scale=1.0 / 128, bias=self.eps_ap),
                 reads=[self.pk[4], "eps"], writes=[sqk])
            P.op("dve", lambda e: e.reciprocal(out=sq[:, :wd], in_=sq[:, :wd]), reads=[sqk], writes=[sqk])
            P.op("dve", lambda e: e.scalar_tensor_tensor(out=qf[:, :wd], in0=pt[:, :wd], scalar=self.qkg[:, gcol:gcol + 1],
                                                         in1=sq[:, :wd], op0=ALU.mult, op1=ALU.mult),
                 reads=[pk, sqk, "qkg"], writes=[qfk])
        else:
            P.op("act", lambda e: e.activation(out=qf[0:M, :wd], in_=pt[0:M, :wd], func=AF.Copy), reads=[pk], writes=[qfk])
        bt, bk = self.br.next()
        if do_rope:
            p3 = self.ps[5]
            rot = self.rot_sb if M == 128 else self.rot64_sb
            yield
            P.op("pe", lambda e: e.matmul(p3[0:M, :wd], lhsT=rot[:], rhs=qf[0:M, :wd], start=True, stop=True),
                 reads=[qfk, "rot", "rot64"], writes=[self.pk[5]])
            t1, t1k = self.fr.next()
            P.op("dve", lambda e: e.tensor_tensor(out=t1[0:M, :wd], in0=p3[0:M, :wd], in1=self.sin_sb[0:M, cs:cs + wd], op=ALU.mult),
                 reads=[self.pk[5], "sin"], writes=[t1k])
            P.op("dve", lambda e: e.tensor_tensor(out=qf[0:M, :wd], in0=qf[0:M, :wd], in1=self.cos_sb[0:M, cs:cs + wd], op=ALU.mult),
                 reads=[qfk, "cos"], writes=[qfk])
            P.op("dve", lambda e: e.tensor_tensor(out=bt[0:M, :wd], in0=qf[0:M, :wd], in1=t1[0:M, :wd], op=ALU.add),
                 reads=[qfk, t1k], writes=[bk])
        else:
            P.op("act", lambda e: e.activation(out=bt[0:M, :wd], in_=qf[0:M, :wd], func=AF.Copy), reads=[qfk], writes=[bk])
        P.dma("sp", dst[drow:drow + M, t0 + s:t0 + s + wd], bt[0:M, :wd], reads=[bk, wkey])

    def load_rope(self, t0, n, half=False):
        P = self.P
        if half:
            P.dma("sp", self.cos_sb[0:64, :n], self.ropeC64[:, t0:t0 + n], writes=["cos"])
            P.dma("sp", self.sin_sb[0:64, :n], self.ropeS64[:, t0:t0 + n], writes=["sin"])
        else:
            P.dma("sp", self.cos_sb[:, :n], self.ropeC[:, t0:t0 + n], writes=["cos"])
            P.dma("sp", self.sin_sb[:, :n], self.ropeS[:, t0:t0 + n], writes=["sin"])

    def stage1_gqa(self, l, kind, src, src_key):
        P = self.P
        Lw = self.L[l]
        md = self.mods(l)
        wkey = ("qkvW", l)
        if kind == 0:
            P.dma("sp", self.small[:, 200:202], Lw["qk_g"], writes=["qkg"])
            self.qkg = self.small[:, 200:202]
        slabs = [(c0, 256, [(0, 128), (128, 128)]) for c0 in range(0, 5120, 256)]
        for sbi, (t0, n, col) in enumerate(SBS):
            hT = self.abuf[:, 0:DC * n].rearrange("p (c t) -> p c t", t=n)
            self.norm_block(src, [(src_key, sbi)], (t0, n), md["A1"], md["B1"], col, hT,
                            [("modv", l), ("A", l, 32)])
            is_ctx = (col == 1)
            do_norm = (kind == 0)
            do_rope = (kind in (0, 1)) and not is_ctx
            if do_rope:
                self.load_rope(t0, n)

            def epi(c, M, s_, wd_, pt, pk, t0=t0, do_norm=do_norm, do_rope=do_rope):
                if c < D:
                    return self.qk_epi_gen(l, do_norm, do_rope, t0, M, s_, wd_, pt, pk, 0, self.qT, c, wkey)
                return self.qk_epi_gen(l, do_norm, do_rope, t0, M, s_, wd_, pt, pk, 1, self.kT, c - D, wkey)
            self.gemm_fm(Lw["w_qkv"], DC, slabs, hT, "abuf", subs_of(n), epi)

            def epi_v(g0, tt, pt, pk, t0=t0):
                bt, bk = self.br.next()
                P.op("act", lambda e: e.activation(out=bt[:, 0:256], in_=pt[:, 0:256], func=AF.Copy), reads=[pk], writes=[bk])
                P.dma("sp", self.vtm[t0 + tt * 128:t0 + (tt + 1) * 128, g0:g0 + 256], bt[:, 0:256], reads=[bk, wkey])
            self.gemm_tm(Lw["w_qkv"], DC, 5120, 1024, hT, "abuf", n // 128, epi_v)
        self.barrier("dve", [wkey])

    def attention(self, l, n_groups, q_per, key_tiles_fn, scale, need_ctx, mla=False, mask_fn=None, sink=False,
                  kv_of=None, group_hook=None):
        P = self.P
        A = self.abuf
        rk = ("qkvW", l)
        wkey = ("oW", l)
        KT = A[:, 0:T]
        VT = A[:, 4352:4352 + 34 * 128].rearrange("p (k d) -> p k d", d=128)
        QT = [A[:, 8704 + i * 2048:8704 + (i + 1) * 2048].rearrange("p (r t) -> p r t", t=512) for i in range(2)]
        KP = A[0:64, 12800:12800 + T]
        QP = [A[0:64, 17152 + i * 512:17152 + (i + 1) * 512] for i in range(2)]
        ptr = Ring("pt", [A[:, 18176 + i * 512:18176 + (i + 1) * 512] for i in range(4)])
        pss = Ring("pss", [self.ps[0], self.ps[1], self.ps[6]], [self.pk[0], self.pk[1], self.pk[6]])
        pso = Ring("pso", self.ps[2:4], self.pk[2:4])
        psd = Ring("psd", self.ps[4:6], self.pk[4:6])
        akeys = ["KT", "VT", ("QT", 0), ("QT", 1), "KP"] + ptr.keys
        self.barrier("dve", ["abuf"] + akeys)
        qblocks = [(t0, 512, 0) for t0 in range(0, NLAT, 512)]
        if need_ctx:
            qblocks.append((NLAT, NCTX, 1))
        if mla:
            P.dma("sp", KP, self.kT[D:D + 64, :], reads=[rk], writes=["KP"])
        work = [(g, qb) for g in range(n_groups) for qb in qblocks]

        def load_q(wi):
            g, (q0, qn, is_ctx) = work[wi]
            qt, qk = QT[wi % 2], ("QT", wi % 2)
            h0 = g * q_per
            P.dma("sp", qt[:, 0:q_per, 0:qn],
                  self.qT[h0 * 128:(h0 + q_per) * 128, q0:q0 + qn].rearrange("(r p) t -> p r t", p=128),
                  reads=[rk], writes=[qk])
            if mla:
                P.dma("sp", QP[wi % 2][:, 0:qn], self.qpT[g * 64:(g + 1) * 64, q0:q0 + qn], reads=[rk], writes=[qk])
        load_q(0)
        LOOK = 2
        jobs = []
        for wi, (g, (q0, qn, is_ctx)) in enumerate(work):
            ktl = key_tiles_fn(q0, is_ctx)
            for r in range(q_per):
                for i, kt in enumerate(ktl):
                    jobs.append((wi, g, r, i, kt, len(ktl)))
        kvf = (lambda g: g) if kv_of is None else kv_of
        nj = len(jobs)
        job_ps = {}
        state = {"qk_next": 0}

        def emit_qk(j):
            wi, g, r, i, kt, n = jobs[j]
            q0, qn, is_ctx = work[wi][1]
            qt, qk, qp = QT[wi % 2], ("QT", wi % 2), QP[wi % 2]
            psx, psk = pss.next()
            job_ps[j] = (psx, psk)
            P.op("pe", lambda e, psx=psx, kt=kt, qt=qt, r=r, qn=qn: e.matmul(
                psx[:, :qn], lhsT=KT[:, kt * 128:(kt + 1) * 128], rhs=qt[:, r, 0:qn], start=True, stop=(not mla)),
                reads=["KT", qk], writes=[psk], track=(not mla))
            if mla:
                P.op("pe", lambda e, psx=psx, kt=kt, qp=qp, qn=qn: e.matmul(
                    psx[:, :qn], lhsT=KP[:, kt * 128:(kt + 1) * 128], rhs=qp[:, 0:qn], start=False, stop=True),
                    reads=["KP", qk], writes=[psk])

        cur = {}
        deferred = []
        for j in range(nj):
            wi, g, r, i, kt, n = jobs[j]
            q0, qn, is_ctx = work[wi][1]
            kvi = kvf(g)
            hh = g * q_per + r
            if j == 0 or jobs[j - 1][0] != wi:
                new_run = (j == 0) or (kvf(jobs[j - 1][1]) != kvi)
                if new_run:
                    assert state["qk_next"] == j
                    P.dma("sp", KT, self.kT[kvi * 128:(kvi + 1) * 128, :], reads=[rk], writes=["KT"])
                    vcol = kvi * 128
                    for kt0 in range(0, 34, 17):
                        P.dma("sp", VT[:, kt0:kt0 + 17, :],
                              self.vtm[kt0 * 128:(kt0 + 17) * 128, vcol:vcol + 128].rearrange("(k p) d -> p k d", p=128),
                              reads=[rk], writes=["VT"])
                if (j == 0 or jobs[j - 1][1] != g) and group_hook is not None:
                    group_hook(g)
                if wi + 1 < len(work):
                    load_q(wi + 1)
                if self.bg is not None:
                    self.bg()
            while (state["qk_next"] < nj and state["qk_next"] <= j + LOOK
                   and kvf(jobs[state["qk_next"]][1]) == kvi):
                emit_qk(state["qk_next"])
                state["qk_next"] += 1
            if i == 0:
                cur["po"], cur["pok"] = pso.next()
                cur["pd"], cur["pdk"] = psd.next()
            po, pok, pd, pdk = cur["po"], cur["pok"], cur["pd"], cur["pdk"]
            psx, psk = job_ps.pop(j)
            pt, ptk = ptr.next()
            P.op("act", lambda e, pt=pt, psx=psx, qn=qn: e.activation(out=pt[:, :qn], in_=psx[:, :qn], func=AF.Exp, scale=scale),
                 reads=[psk], writes=[ptk])
            if mask_fn is not None:
                for (mk_ap, mk_key) in mask_fn(hh, q0, kt, is_ctx):
                    P.op("dve", lambda e, pt=pt, mk_ap=mk_ap, qn=qn: e.tensor_tensor(out=pt[:, :qn], in0=pt[:, :qn], in1=mk_ap[:, :qn], op=ALU.mult),
                         reads=[ptk, mk_key], writes=[ptk])
            last = (i == n - 1)
            P.op("pe", lambda e, po=po, pt=pt, kt=kt, qn=qn, i=i, last=last: e.matmul(
                po[:, :qn], lhsT=VT[:, kt, :], rhs=pt[:, :qn], start=(i == 0), stop=last),
                reads=["VT", ptk], writes=[pok], track=False)
            P.op("pe", lambda e, pd=pd, pt=pt, qn=qn, i=i, last=last: e.matmul(
                pd[:, :qn], lhsT=self.ones_b[:], rhs=pt[:, :qn], start=(i == 0), stop=last),
                reads=["ones_b", ptk], writes=[pdk], track=True)
            if deferred and (i >= 1 or last):
                for fn_ in deferred:
                    fn_()
                del deferred[:]
            if last:
                def head_epilogue(po=po, pok=pok, pd=pd, pdk=pdk, qn=qn, hh=hh, q0=q0):
                    ft, fk = self.fr.next()
                    if sink:
                        P.op("dve", lambda e, ft=ft, pd=pd, qn=qn, hh=hh: e.tensor_scalar(
                            out=ft[:, :qn], in0=pd[:, :qn], scalar1=self.esink[:, hh:hh + 1], scalar2=None, op0=ALU.add),
                            reads=[pdk, "esink"], writes=[fk])
                        P.op("dve", lambda e, ft=ft, qn=qn: e.reciprocal(out=ft[:, :qn], in_=ft[:, :qn]), reads=[fk], writes=[fk])
                    else:
                        P.op("dve", lambda e, ft=ft, pd=pd, qn=qn: e.reciprocal(out=ft[:, :qn], in_=pd[:, :qn]), reads=[pdk], writes=[fk])
                    bt, bk = self.br.next()
                    P.op("dve", lambda e, bt=bt, po=po, ft=ft, qn=qn: e.tensor_tensor(out=bt[:, :qn], in0=po[:, :qn], in1=ft[:, :qn], op=ALU.mult),
                         reads=[pok, fk], writes=[bk])
                    P.dma("sp", self.oT[hh * 128:(hh + 1) * 128, q0:q0 + qn], bt[:, :qn], reads=[bk, wkey])
                deferred.append(head_epilogue)
        for fn_ in deferred:
            fn_()
        self.barrier("dve", [wkey, "abuf"] + akeys)


    def attention_window(self, l, need_ctx):
        P = self.P
        Lw = self.L[l]
        A = self.abuf
        wm = A[:, 20480:20480 + 6 * 512].rearrange("p (i q) -> p i q", q=512)
        self.barrier("dve", ["abuf", "wm"])
        P.dma("pool", wm, self.wmask.rearrange("i p q -> p i q"), writes=["wm"])
        P.dma("sp", self.small[:, 210:242], Lw["sink"], writes=["esink"])
        self.esink = self.small[:, 210:242]
        P.op("act", lambda e: e.activation(out=self.small[:, 210:242], in_=self.small[:, 210:242], func=AF.Exp),
             reads=["esink"], writes=["esink"])

        def ktiles(q0, is_ctx):
            if is_ctx:
                return [32, 33]
            b = q0 // 128
            return [32, 33] + [kt for kt in range(b - 1, b + 5) if 0 <= kt < 32]

        def mask_fn(hh, q0, kt, is_ctx):
            if is_ctx or kt >= 32:
                return []
            di = kt - q0 // 128 + 1
            return [(wm[:, di, :], "wm")]
        self.attention(l, 8, 4, ktiles, 128 ** -0.5, need_ctx, mask_fn=mask_fn, sink=True)
        self.barrier("dve", ["wm", "abuf"])

    def attention_nbr(self, l, need_ctx):
        P = self.P
        Lw = self.L[l]
        A = self.abuf
        NM = A[:, 20480:20480 + 24 * 512].rearrange("p (i q) -> p i q", q=512)
        EH = A[:, 12800:12800 + 8 * 512].rearrange("p (i q) -> p i q", q=512)
        self.barrier("dve", ["abuf", "EH", "NM", "KP"])
        P.dma("pool", NM, self.nmask.rearrange("t k p q -> p (t k) q"), writes=["NM"])

        def hook(h):
            for kr in range(8):
                ft, fk = self.fr.next()
                P.dma("sp", ft[:, :], Lw["relb"][h, kr], writes=[fk])
                P.op("act", lambda e, ft=ft, kr=kr: e.activation(out=EH[:, kr, :], in_=ft[:, :], func=AF.Exp),
                     reads=[fk], writes=["EH"])

        def ktiles(q0, is_ctx):
            if is_ctx:
                return [32, 33]
            j = q0 // 512
            return [32, 33] + [kt for kt in range(4 * j - 2, 4 * j + 6) if 0 <= kt < 32]

        def mask_fn(hh, q0, kt, is_ctx):
            if is_ctx or kt >= 32:
                return []
            j = q0 // 512
            ty = 0 if j == 0 else (2 if j == 7 else 1)
            kr = kt - (4 * j - 2)
            return [(EH[:, kr, :], "EH"), (NM[:, ty * 8 + kr, :], "NM")]
        self.attention(l, 32, 1, ktiles, 128 ** -0.5, need_ctx, mask_fn=mask_fn, kv_of=lambda g: g // 4, group_hook=hook)
        self.barrier("dve", ["EH", "NM", "KP", "abuf"])

    def stage1_mla(self, l, src, src_key):
        P = self.P
        Lw = self.L[l]
        md = self.mods(l)
        wkey = ("qkvW", l)
        P.dma("sp", self.small[:, 200:208], Lw["q_g"], writes=["mlag"])
        P.dma("sp", self.small[:, 208:212], Lw["kv_g"], writes=["mlag"])
        qg = self.small[:, 200:208]
        kvg = self.small[:, 208:212]
        mb = self.mbuf
        ps6 = self.ps[6]
        for sbi, (t0, n, col) in enumerate(SBS):
            hT = self.abuf[:, 0:DC * n].rearrange("p (c t) -> p c t", t=n)
            self.norm_block(src, [(src_key, sbi)], (t0, n), md["A1"], md["B1"], col, hT, [("modv", l), ("A", l, 32)])
            is_ctx = (col == 1)
            do_rope = not is_ctx
            if do_rope:
                self.load_rope(t0, n, half=True)
            for (s0, wd0) in subs_of(n, 256):
                cqn = self.cqn[:, :, 0:wd0]
                ckvn = self.ckvn[:, :, 0:wd0]

                def low_rank(w, ncol, nch, gtile, dstn, dkey, rope_tail, s0=s0, wd0=wd0, t0=t0, do_rope=do_rope, hT=hT):
                    def epi(c, M, s_, wd_, pt, pk):
                        j = c // 128
                        if c >= ncol:
                            self.qk_epilogue(l, False, do_rope, t0, M, s0, wd0, pt, pk, 0, self.kT, D, wkey, cs=s0)
                            return
                        P.op("act", lambda e: e.activation(out=mb[:, j, 0:wd0], in_=pt[:, 0:wd0], func=AF.Copy), reads=[pk], writes=[("mb", j)])
                        ft, fk = self.fr.next()
                        P.op("act", lambda e: e.activation(out=ft[:, 0:wd0], in_=pt[:, 0:wd0], func=AF.Square), reads=[pk], writes=[fk])
                        P.op("pe", lambda e: e.matmul(ps6[:, 0:wd0], lhsT=self.ones_f[:], rhs=ft[:, 0:wd0], start=(j == 0), stop=(j == nch - 1)),
                             reads=[fk, "ones_f"], writes=[self.pk[6]])
                    slabs = [(c0, 256, [(0, 128), (128, 128)]) for c0 in range(0, ncol, 256)]
                    if rope_tail:
                        slabs.append((ncol, 64, [(0, 64)]))
                    self.gemm_fm(w, DC, slabs, hT, "abuf", [(s0, wd0)], epi)
                    ft, fk = self.fr.next()
                    P.op("act", lambda e: e.activation(out=ft[:, 0:wd0], in_=ps6[:, 0:wd0], func=AF.Sqrt, scale=1.0 / ncol, bias=self.eps_ap),
                         reads=[self.pk[6], "eps"], writes=[fk])
                    P.op("dve", lambda e: e.reciprocal(out=self.rstd[:, 0:wd0], in_=ft[:, 0:wd0]), reads=[fk], writes=["rstd"])
                    for j in range(nch):
                        P.op("dve", lambda e, j=j: e.scalar_tensor_tensor(out=dstn[:, j, 0:wd0], in0=mb[:, j, 0:wd0], scalar=gtile[:, j:j + 1],
                                                                     in1=self.rstd[:, 0:wd0], op0=ALU.mult, op1=ALU.mult),
                             reads=[("mb", j), "rstd", "mlag"], writes=[dkey])
                low_rank(Lw["w_dq"], 1024, 8, qg, cqn, "cqn", False)
                low_rank(Lw["w_dkv"], 512, 4, kvg, ckvn, "ckvn", True)

                def epi_q(c, M, s_, wd_, pt, pk, s0=s0, wd0=wd0, t0=t0, do_rope=do_rope):
                    h = c // 192
                    if M == 128:
                        return self.qk_epi_gen(l, False, False, t0, M, s0, wd0, pt, pk, 0, self.qT, h * 128, wkey, cs=s0)
                    return self.qk_epi_gen(l, False, do_rope, t0, M, s0, wd0, pt, pk, 0, self.qpT, h * 64, wkey, cs=s0)
                slabs = [(c0, 384, [(0, 128), (128, 64), (192, 128), (320, 64)]) for c0 in range(0, 6144, 384)]
                self.gemm_fm(Lw["w_uq"], 8, slabs, cqn, "cqn", [(0, wd0)], epi_q)

                wv_d = Lw["w_ukv"].rearrange("(c p) n -> p c n", p=128)
                for c0 in range(0, 8192, 512):
                    h2 = c0 // 256
                    wb, wk = self.wbuf.next()
                    wv = wb[:, 0:4 * 512].rearrange("p (c n) -> p c n", n=512)
                    P.dma("pool", wv, wv_d[:, :, c0:c0 + 512], writes=[wk])
                    for hd in range(2):
                        pt, pk = self.psr.next()
                        for k in range(4):
                            P.op("pe", lambda e, pt=pt, wv=wv, k=k, hd=hd, ckvn=ckvn, wd0=wd0: e.matmul(
                                pt[:, 0:wd0], lhsT=wv[:, k, hd * 256:hd * 256 + 128], rhs=ckvn[:, k, 0:wd0], start=(k == 0), stop=(k == 3)),
                                reads=[wk, "ckvn"], writes=[pk], track=(k == 3))
                        self.qk_epilogue(l, False, False, t0, 128, s0, wd0, pt, pk, 0, self.kT, (h2 + hd) * 128, wkey, cs=s0)
                    for tt in range(wd0 // 128):
                        pt, pk = self.psr.next()
                        for hd in range(2):
                            for k in range(4):
                                P.op("pe", lambda e, pt=pt, wv=wv, k=k, hd=hd, tt=tt, ckvn=ckvn: e.matmul(
                                    pt[:, hd * 128:(hd + 1) * 128], lhsT=ckvn[:, k, tt * 128:(tt + 1) * 128],
                                    rhs=wv[:, k, hd * 256 + 128:hd * 256 + 256], start=(k == 0), stop=(k == 3)),
                                    reads=[wk, "ckvn"], writes=[pk], track=(k == 3 and hd == 1))
                        bt, bk = self.br.next()
                        P.op("act", lambda e, bt=bt, pt=pt: e.activation(out=bt[:, 0:256], in_=pt[:, 0:256], func=AF.Copy), reads=[pk], writes=[bk])
                        tok = t0 + s0 + tt * 128
                        P.dma("sp", self.vtm[tok:tok + 128, h2 * 128:(h2 + 2) * 128], bt[:, 0:256], reads=[bk, wkey])
        self.barrier("dve", [wkey])

    def stage3_oproj(self, l, src, src_key, dst, dst_key, need_ctx):
        P = self.P
        Lw = self.L[l]
        md = self.mods(l)
        slabs = [(c0, 256, [(0, 128), (128, 128)]) for c0 in range(0, D, 256)]
        for sbi, (t0, n, col) in enumerate(SBS):
            if col == 1 and not need_ctx:
                continue
            oT_sb = self.abuf[:, 0:DC * n].rearrange("p (c t) -> p c t", t=n)
            self.barrier("dve", ["abuf"])
            for c0 in range(0, DC, 8):
                P.dma("sp", oT_sb[:, c0:c0 + 8, :], self.oT[c0 * 128:(c0 + 8) * 128, t0:t0 + n].rearrange("(c p) t -> p c t", p=128),
                      reads=[("oW", l), "abuf"])
            self.barrier("dve", ["abuf"])
            self.resid_gemm(Lw["w_o"], DC, slabs, oT_sb, subs_of(n), 0, md["G1"], col, src, src_key, dst, dst_key, sbi, t0,
                            [("modv", l)])
            self.barrier("dve", [(dst_key, sbi)])

    def resid_gemm(self, w, KC, slabs, act, subs, soff, G, col, src, src_key, dst, dst_key, sbi, t0, modkeys):
        P = self.P

        def epi(c, M, s, wd, pt, pk):
            j = c // 128
            xt, xk = self.xr.next()
            tt = t0 + soff + s
            P.dma("sp", xt[:, :wd], src[c:c + 128, tt:tt + wd], reads=[(src_key, sbi)], writes=[xk])
            ft, fk = self.fr.next()
            P.op("dve", lambda e: e.scalar_tensor_tensor(out=ft[:, :wd], in0=pt[:, :wd], scalar=G[:, j, col:col + 1],
                                                         in1=xt[:, :wd], op0=ALU.mult, op1=ALU.add),
                 reads=[pk, xk] + modkeys, writes=[fk])
            P.dma("sp", dst[c:c + 128, tt:tt + wd], ft[:, :wd], reads=[fk, (dst_key, sbi)])
        self.gemm_fm(w, KC, slabs, act, "abuf", subs, epi)

    def stage4_ffn(self, l, src, src_key, dst, dst_key, need_ctx):
        P = self.P
        Lw = self.L[l]
        md = self.mods(l)
        P.dma("sp", self.cw[:], Lw["conv_w"], writes=["cw"])
        P.dma("sp", self.cb[:], Lw["conv_b"], writes=["cb"])
        wv_d = Lw["w_up"].rearrange("(c p) n -> p c n", p=128)
        abf, vbf = self.apad, self.vpad
        for sbi, (t0, n, col) in enumerate(SBS):
            if col == 1 and not need_ctx:
                continue
            lo = 1 if (col == 0 and t0 > 0) else 0
            hi = 1 if (col == 0 and t0 + n < NLAT) else 0
            ne = n + lo + hi
            hT = self.abuf[:, 0:DC * ne].rearrange("p (c t) -> p c t", t=ne)
            skeys = [(src_key, sbi)] + ([(src_key, sbi - 1)] if lo else []) + ([(src_key, sbi + 1)] if hi else [])
            self.norm_block(src, skeys, (t0 - lo, ne), md["A2"], md["B2"], col, hT, [("modv", l), ("A", l, 128)])
            ukey = ("uT", l, sbi)
            P.op("dve", lambda e: e.memset(abf[:, 0:1], 0.0), writes=["apad"])
            P.op("dve", lambda e, ne=ne: e.memset(abf[:, ne + 1:ne + 2], 0.0), writes=["apad"])
            subs_e = subs_of(ne)
            for m2 in range(0, FC, 2):
                wa, wak = self.wbuf.next()
                wav = wa[:, 0:DC * 256].rearrange("p (c n) -> p c n", n=256)
                P.dma("pool", wav, wv_d[:, :, m2 * 128:(m2 + 2) * 128], writes=[wak])
                wg, wgk = self.wbuf.next()
                wgv = wg[:, 0:DC * 256].rearrange("p (c n) -> p c n", n=256)
                P.dma("pool", wgv, wv_d[:, :, DFF + m2 * 128:DFF + (m2 + 2) * 128], writes=[wgk])
                for mi in range(2):
                    m = m2 + mi
                    for (s, wd) in subs_e:
                        pa, pak = self.psr.next()
                        for k in range(DC):
                            P.op("pe", lambda e, pa=pa, wav=wav, mi=mi, k=k, s=s, wd=wd, hT=hT: e.matmul(
                                pa[:, :wd], lhsT=wav[:, k, mi * 128:(mi + 1) * 128], rhs=hT[:, k, s:s + wd],
                                start=(k == 0), stop=(k == DC - 1)),
                                reads=[wak, "abuf"], writes=[pak], track=(k == DC - 1))
                        P.op("act", lambda e, pa=pa, s=s, wd=wd: e.activation(out=abf[:, 1 + s:1 + s + wd], in_=pa[:, :wd], func=AF.Copy),
                             reads=[pak], writes=["apad"])
                        pv, pvk = self.psr.next()
                        for k in range(DC):
                            P.op("pe", lambda e, pv=pv, wgv=wgv, mi=mi, k=k, s=s, wd=wd, hT=hT: e.matmul(
                                pv[:, :wd], lhsT=wgv[:, k, mi * 128:(mi + 1) * 128], rhs=hT[:, k, s:s + wd],
                                start=(k == 0), stop=(k == DC - 1)),
                                reads=[wgk, "abuf"], writes=[pvk], track=(k == DC - 1))
                        P.op("act", lambda e, pv=pv, s=s, wd=wd: e.activation(out=vbf[:, s:s + wd], in_=pv[:, :wd], func=AF.Copy),
                             reads=[pvk], writes=["vpad"])
                    if lo == 0:
                        P.op("dve", lambda e: e.memset(abf[:, 0:1], 0.0), writes=["apad"])
                    if hi == 0:
                        P.op("dve", lambda e, ne=ne: e.memset(abf[:, ne + 1:ne + 2], 0.0), writes=["apad"])
                    for (s, wd) in subs_of(n):
                        e0 = lo + s
                        ft, fk = self.fr.next()
                        P.op("dve", lambda e, ft=ft, e0=e0, wd=wd, m=m: e.tensor_scalar(
                            out=ft[:, :wd], in0=abf[:, e0:e0 + wd], scalar1=self.cw[:, m, 0:1], scalar2=self.cb[:, m:m + 1],
                            op0=ALU.mult, op1=ALU.add), reads=["apad", "cw", "cb"], writes=[fk])
                        P.op("dve", lambda e, ft=ft, e0=e0, wd=wd, m=m: e.scalar_tensor_tensor(
                            out=ft[:, :wd], in0=abf[:, e0 + 1:e0 + 1 + wd], scalar=self.cw[:, m, 1:2], in1=ft[:, :wd],
                            op0=ALU.mult, op1=ALU.add), reads=["apad", "cw", fk], writes=[fk])
                        P.op("dve", lambda e, ft=ft, e0=e0, wd=wd, m=m: e.scalar_tensor_tensor(
                            out=ft[:, :wd], in0=abf[:, e0 + 2:e0 + 2 + wd], scalar=self.cw[:, m, 2:3], in1=ft[:, :wd],
                            op0=ALU.mult, op1=ALU.add), reads=["apad", "cw", fk], writes=[fk])
                        f2, f2k = self.fr.next()
                        P.op("act", lambda e, ft=ft, f2=f2, wd=wd: e.activation(out=f2[:, :wd], in_=ft[:, :wd], func=AF.Silu),
                             reads=[fk], writes=[f2k])
                        bt, bk = self.br.next()
                        P.op("dve", lambda e, bt=bt, f2=f2, e0=e0, wd=wd: e.tensor_tensor(
                            out=bt[:, :wd], in0=f2[:, :wd], in1=vbf[:, e0:e0 + wd], op=ALU.mult),
                            reads=[f2k, "vpad"], writes=[bk])
                        P.dma("sp", self.uT[m * 128:(m + 1) * 128, t0 + s:t0 + s + wd], bt[:, :wd], reads=[bk, ukey])
            self.barrier("dve", [ukey])
            slabs = [(c0, 256, [(0, 128), (128, 128)]) for c0 in range(0, D, 256)]
            for (s, wd) in subs_of(n):
                u_sb = self.abuf[:, 0:FC * wd].rearrange("p (c t) -> p c t", t=wd)
                self.barrier("dve", ["abuf"])
                for c0 in range(0, FC, 8):
                    P.dma("sp", u_sb[:, c0:c0 + 8, :],
                          self.uT[c0 * 128:(c0 + 8) * 128, t0 + s:t0 + s + wd].rearrange("(c p) t -> p c t", p=128),
                          reads=[ukey, "abuf"])
                self.barrier("dve", ["abuf"])
                self.resid_gemm(Lw["w_down"], FC, slabs, u_sb, [(0, wd)], s, md["G2"], col, src, src_key, dst, dst_key, sbi, t0,
                                [("modv", l)])
            self.barrier("dve", [(dst_key, sbi)])

    def final_norm(self, src, src_key):
        P = self.P
        P.dma("sp", self.g_sb[:], self.fin_g, writes=["g_sb"])
        ps = self.ps[6]
        for sbi, (t0, n, col) in enumerate(SBS):
            if col == 1:
                continue
            for (s, w) in subs_of(n):
                for j in range(DC):
                    xt, xk = self.xr.next()
                    P.dma("sp", xt[:, :w], src[j * 128:(j + 1) * 128, t0 + s:t0 + s + w], reads=[(src_key, sbi)], writes=[xk])
                    ft, fk = self.fr.next()
                    P.op("act", lambda e, ft=ft, xt=xt, w=w: e.activation(out=ft[:, :w], in_=xt[:, :w], func=AF.Square), reads=[xk], writes=[fk])
                    P.op("pe", lambda e, ft=ft, w=w, j=j: e.matmul(ps[:, :w], lhsT=self.ones_f[:], rhs=ft[:, :w], start=(j == 0), stop=(j == DC - 1)),
                         reads=[fk, "ones_f"], writes=[self.pk[6]])
                ft, fk = self.fr.next()
                P.op("act", lambda e, ft=ft, w=w: e.activation(out=ft[:, :w], in_=ps[:, :w], func=AF.Sqrt, scale=1.0 / D, bias=self.eps_ap),
                     reads=[self.pk[6], "eps"], writes=[fk])
                P.op("dve", lambda e, ft=ft, w=w: e.reciprocal(out=self.rstd[:, :w], in_=ft[:, :w]), reads=[fk], writes=["rstd"])
                for j in range(DC):
                    xt, xk = self.xr.next()
                    P.dma("sp", xt[:, :w], src[j * 128:(j + 1) * 128, t0 + s:t0 + s + w], reads=[(src_key, sbi)], writes=[xk])
                    ft, fk = self.fr.next()
                    P.op("dve", lambda e, ft=ft, xt=xt, w=w, j=j: e.scalar_tensor_tensor(
                        out=ft[:, :w], in0=xt[:, :w], scalar=self.g_sb[:, j:j + 1], in1=self.rstd[:, :w], op0=ALU.mult, op1=ALU.mult),
                        reads=[xk, "rstd", "g_sb"], writes=[fk])
                    P.dma("sp", self.out[j * 128:(j + 1) * 128, t0 + s:t0 + s + w], ft[:, :w], reads=[fk])

    def dump(self, src, src_key, ncols=NLAT, c0=0):
        P = self.P
        allk = [(src_key, i) for i in range(len(SBS))]
        for (s, w) in subs_of(ncols):
            for j in range(DC):
                xt, xk = self.xr.next()
                P.dma("sp", xt[:, :w], src[j * 128:(j + 1) * 128, c0 + s:c0 + s + w], reads=allk, writes=[xk])
                P.dma("sp", self.out[j * 128:(j + 1) * 128, s:s + w], xt[:, :w], reads=[xk])

    def build(self):
        P = self.P
        self.esink = None
        self.setup_consts()
        self.phase_mod(0)
        cur, cur_key = self.xT, "xin"
        if self.stop_after == "mod":
            P.dma("sp", self.out[0:128, 0:384], self.modv[0][:].rearrange("p m j -> p (m j)"), reads=[("modv", 0)])
            P.dma("sp", self.out[128:256, 0:64], self.A1[0][:].rearrange("p m j -> p (m j)"), reads=[("A", 0, 32)])
            P.finish()
            return P
        for l in range(self.n_layers):
            kind = l % 4
            need_ctx = l < 3
            self.barrier("dve", [])
            self.barrier("act", [])
            if l > 0:
                P.switch_sems()
            if l == 0:
                self.bg_sched = [(l2, 3) for l2 in (1, 2) if l2 < self.n_layers]
            elif l == 2:
                self.bg_sched = [(l2, 1) for l2 in (3,) if l2 < self.n_layers]
            else:
                self.bg_sched = []
            self.bg = self.bg_step if self.bg_sched else None
            if kind == 0:
                self.stage1_gqa(l, kind, cur, cur_key)
                self.attention(l, 8, 4, lambda q0, is_ctx: ([32, 33] if is_ctx else [32, 33] + list(range(32))),
                               128 ** -0.5, need_ctx)
            elif kind == 1:
                self.stage1_gqa(l, kind, cur, cur_key)
                self.attention_window(l, need_ctx)
            elif kind == 2:
                self.stage1_gqa(l, kind, cur, cur_key)
                self.attention_nbr(l, need_ctx)
            else:
                self.stage1_mla(l, cur, cur_key)
                self.attention(l, 32, 1, lambda q0, is_ctx: ([32, 33] if is_ctx else [32, 33] + list(range(32))),
                               192 ** -0.5, need_ctx, mla=True)
            self.bg = None
            for (l2, _) in self.bg_sched:
                self.phase_mod(l2)
            self.bg_sched = []
            if l + 1 < self.n_layers:
                self.phase_mod(l + 1)
            if self.stop_after in ("s1", "attn"):
                if self.stop_after == "s1":
                    srcs = [(self.qT, 0, 1024, ("qkvW", l)), (self.kT, 1024, 1024, ("qkvW", l))]
                else:
                    srcs = [(self.oT, 0, 2048, ("oW", l))]
                for (sd, o0, nr, key) in srcs:
                    for r0 in range(0, nr, 128):
                        for (s_, w_) in subs_of(NLAT):
                            bt, bk = self.br.next()
                            P.dma("sp", bt[:, :w_], sd[r0:r0 + 128, s_:s_ + w_], reads=[key], writes=[bk])
                            ft, fk = self.fr.next()
                            P.op("act", lambda e, ft=ft, bt=bt, w_=w_: e.activation(out=ft[:, :w_], in_=bt[:, :w_], func=AF.Copy), reads=[bk], writes=[fk])
                            P.dma("sp", self.out[o0 + r0:o0 + r0 + 128, s_:s_ + w_], ft[:, :w_], reads=[fk])
                P.finish()
                return P
            self.stage3_oproj(l, cur, cur_key, self.xm, "xm", need_ctx)
            self.stage4_ffn(l, self.xm, "xm", self.xs, "xs", need_ctx)
            cur, cur_key = self.xs, "xs"
        if self.debug_out == "x":
            self.dump(cur, cur_key)
        elif self.debug_out == "cx":
            self.dump(cur, cur_key, NCTX, NLAT)
        else:
            self.final_norm(cur, cur_key)
        P.finish()
        return P


def _chunk_layout(v):
    v = np.asarray(v, np.float32)
    return np.ascontiguousarray(v.reshape(-1, 128).T)


def _rope_tables(rot_dim):
    n_freq = rot_dim // 4
    freqs = (10000.0 ** (-np.arange(n_freq, dtype=np.float64) / n_freq))
    t = np.arange(NLAT)
    row = (t // 64).astype(np.float64)
    colp = (t % 64).astype(np.float64)
    ang = np.concatenate([row[:, None] * freqs, colp[:, None] * freqs], axis=-1)
    ang = ang.astype(np.float32).astype(np.float64)
    cos = np.cos(ang).T
    sin = np.sin(ang).T
    C = np.concatenate([cos, cos], 0)
    S = np.concatenate([sin, sin], 0)
    C = np.concatenate([C, np.ones((rot_dim, NCTX))], 1)
    S = np.concatenate([S, np.zeros((rot_dim, NCTX))], 1)
    return np.ascontiguousarray(C, np.float32), np.ascontiguousarray(S, np.float32)


def _rot_matrix(n):
    h = n // 2
    R = np.zeros((n, n), np.float32)
    for m in range(h):
        R[m + h, m] = -1.0
        R[m, m + h] = 1.0
    return R


INPUT_NAMES = (
    "x", "c", "ctx", "c_ctx", "final_norm_g",
    "l0_ada_w", "l0_ada_b", "l0_attn_norm_g", "l0_w_qkv", "l0_q_norm_g", "l0_k_norm_g", "l0_w_o",
    "l0_ffn_norm_g", "l0_ffn_w_up", "l0_ffn_conv_w", "l0_ffn_conv_b", "l0_ffn_w_down",
    "l1_ada_w", "l1_ada_b", "l1_attn_norm_g", "l1_w_qkv", "l1_sink", "l1_w_o",
    "l1_ffn_norm_g", "l1_ffn_w_up", "l1_ffn_conv_w", "l1_ffn_conv_b", "l1_ffn_w_down",
    "l2_ada_w", "l2_ada_b", "l2_attn_norm_g", "l2_w_qkv", "l2_rel_bias", "l2_w_o",
    "l2_ffn_norm_g", "l2_ffn_w_up", "l2_ffn_conv_w", "l2_ffn_conv_b", "l2_ffn_w_down",
    "l3_ada_w", "l3_ada_b", "l3_attn_norm_g", "l3_w_dq", "l3_q_norm_g", "l3_w_uq", "l3_w_dkv",
    "l3_kv_norm_g", "l3_w_ukv", "l3_w_o",
    "l3_ffn_norm_g", "l3_ffn_w_up", "l3_ffn_conv_w", "l3_ffn_conv_b", "l3_ffn_w_down",
)


def host_inputs(inputs):
    missing = [n for n in INPUT_NAMES if n not in inputs]
    assert not missing, missing
    shared = {}
    C, S = _rope_tables(128)
    shared["ropeC"], shared["ropeS"] = C, S
    C64, S64 = _rope_tables(64)
    shared["ropeC64"], shared["ropeS64"] = C64, S64
    shared["rotm"] = _rot_matrix(128)
    shared["rotm64"] = _rot_matrix(64)
    kk = np.arange(128)[:, None]
    qq = np.arange(512)[None, :]
    shared["wmask"] = np.stack([(np.abs(qq - kk - (di - 1) * 128) <= 128) for di in range(6)], 0).astype(np.float32)
    krl, kc = np.arange(128)[:, None] // 64, np.arange(128)[:, None] % 64
    qrl, qc = np.arange(512)[None, :] // 64, np.arange(512)[None, :] % 64
    nm = np.zeros((3, 8, 128, 512), np.float32)
    NB_DR = np.zeros((8, 128, 512), np.int64)
    NB_DC = np.zeros((8, 128, 512), np.int64)
    for ty, j in enumerate((0, 3, 7)):
        for kr_ in range(8):
            krow = 8 * j - 4 + 2 * kr_ + krl
            qrow = 8 * j + qrl
            rs = np.clip(qrow - 4, 0, 56)
            cs_ = np.clip(qc - 8, 0, 48)
            inw = (krow >= rs) & (krow < rs + 8) & (kc >= cs_) & (kc < cs_ + 16) & (krow >= 0) & (krow < 64)
            nm[ty, kr_] = inw
            if ty == 1:
                NB_DR[kr_] = np.clip(krow - qrow + 7, 0, 14)
                NB_DC[kr_] = np.clip(kc - qc + 15, 0, 30)
    shared["nmask"] = nm
    shared["fin_g"] = _chunk_layout(inputs["final_norm_g"])
    for l in range(4):
        p = "l%d_" % l
        shared[p + "ada_w"] = np.asarray(inputs[p + "ada_w"], np.float32)
        shared[p + "ada_b"] = _chunk_layout(inputs[p + "ada_b"])
        shared[p + "attn_g"] = _chunk_layout(inputs[p + "attn_norm_g"])
        shared[p + "ffn_g"] = _chunk_layout(inputs[p + "ffn_norm_g"])
        if l < 3:
            shared[p + "w_qkv"] = np.asarray(inputs[p + "w_qkv"], np.float32)
        if l == 0:
            shared[p + "qk_g"] = np.ascontiguousarray(np.stack([inputs[p + "q_norm_g"], inputs[p + "k_norm_g"]], 1), np.float32)
        if l == 1:
            shared[p + "sink"] = np.ascontiguousarray(np.broadcast_to(np.asarray(inputs[p + "sink"], np.float32)[None, :], (128, 32)))
        if l == 2:
            rb = np.asarray(inputs[p + "rel_bias"], np.float32)
            shared[p + "relb"] = np.ascontiguousarray(rb[:, NB_DR, NB_DC])
        if l == 3:
            shared[p + "w_dq"] = np.asarray(inputs[p + "w_dq"], np.float32)
            shared[p + "q_g"] = _chunk_layout(inputs[p + "q_norm_g"])
            shared[p + "w_uq"] = np.asarray(inputs[p + "w_uq"], np.float32)
            shared[p + "w_dkv"] = np.asarray(inputs[p + "w_dkv"], np.float32)
            shared[p + "kv_g"] = _chunk_layout(inputs[p + "kv_norm_g"])
            shared[p + "w_ukv"] = np.asarray(inputs[p + "w_ukv"], np.float32)
        shared[p + "w_o"] = np.asarray(inputs[p + "w_o"], np.float32)
        shared[p + "w_up"] = np.asarray(inputs[p + "ffn_w_up"], np.float32)
        cw = np.asarray(inputs[p + "ffn_conv_w"], np.float32)
        shared[p + "conv_w"] = np.ascontiguousarray(cw.reshape(3, FC, 128).transpose(2, 1, 0))
        shared[p + "conv_b"] = _chunk_layout(inputs[p + "ffn_conv_b"])
        shared[p + "w_down"] = np.asarray(inputs[p + "ffn_w_down"], np.float32)
    maps = []
    x = np.asarray(inputs["x"], np.float32)
    ctx = np.asarray(inputs["ctx"], np.float32)
    c = np.asarray(inputs["c"], np.float32)
    cc = np.asarray(inputs["c_ctx"], np.float32)
    for b in range(NCORES):
        m = dict(shared)
        m["xT"] = np.ascontiguousarray(np.concatenate([x[b], ctx[b]], 0).T)
        m["cvec"] = np.ascontiguousarray(np.stack([_chunk_layout(c[b]), _chunk_layout(cc)], -1))
        maps.append(m)
    return maps


def kernel(**inputs):
    B = Builder(n_layers=4)
    P = B.build()
    maps = host_inputs(inputs)
    maps = [{k: v for k, v in m.items() if k in B.inp} for m in maps]
    res = run_bass_kernel_spmd(P.nc, maps, core_ids=list(range(NCORES)))
    out = np.stack([np.ascontiguousarray(res.results[b]["outT"].T) for b in range(NCORES)], 0)
    P.es.close()
    return out.astype(np.float32)
```
